# Optimizing a Trainium2 kernel written in Bass

```python
import jax, jax.numpy as jnp
from jax import lax
import numpy as np

D_MODEL = 2048
BATCH = 8
SEQ = 2048
DEPTH = 2

GRID_W = 64
HEAD_DIM = 128
A_HEADS = 8
A_KV_HEADS = 2
B_HEADS = 4
B_Q_LORA = 512
B_KV_LORA = 256
B_NOPE = 128
B_ROPE = 64
B_V = 128
C_HEADS = 4
WIN_ROWS = 8
WIN_COLS = 16
D_MIX = A_HEADS * HEAD_DIM + B_HEADS * B_V + C_HEADS * HEAD_DIM
Q_BLOCK = 128
ROPE_THETA = 10000.0
NORM_EPS = 1e-6
IN_SPLITS = (
    A_HEADS * HEAD_DIM,
    A_KV_HEADS * HEAD_DIM,
    A_KV_HEADS * HEAD_DIM,
    A_HEADS * HEAD_DIM,
    B_Q_LORA,
    B_KV_LORA,
    B_ROPE,
    B_HEADS * B_V,
    C_HEADS * HEAD_DIM,
    C_HEADS * HEAD_DIM,
    C_HEADS * HEAD_DIM,
    C_HEADS * HEAD_DIM,
)
D_IN = sum(IN_SPLITS)

kernel_name = "hybrid_gqa_mla_natten_encoder"


def _rms_norm(x, g):
    xf = x.astype(jnp.float32)
    y = xf * lax.rsqrt(jnp.mean(xf * xf, axis=-1, keepdims=True) + NORM_EPS)
    return (y * g.astype(jnp.float32)).astype(x.dtype)


def _rope_half(x, pos):
    n = x.shape[-1]
    inv_freq = 1.0 / (ROPE_THETA ** (jnp.arange(0, n, 2, dtype=jnp.float32) / n))
    ang = pos.astype(jnp.float32)[:, None] * inv_freq[None, :]
    cos = jnp.cos(ang)[:, None, :]
    sin = jnp.sin(ang)[:, None, :]
    xf = x.astype(jnp.float32)
    x1, x2 = xf[..., : n // 2], xf[..., n // 2:]
    out = jnp.concatenate([x1 * cos - x2 * sin, x2 * cos + x1 * sin], axis=-1)
    return out.astype(x.dtype)


def _rope_axial(x, row, col):
    h = x.shape[-1] // 2
    return jnp.concatenate([_rope_half(x[..., :h], row), _rope_half(x[..., h:], col)], axis=-1)


def _dense_block_attention(q, k, v):
    B, S, H, dq = q.shape
    Hk = k.shape[2]
    G = H // Hk
    dv = v.shape[-1]
    nb = S // Q_BLOCK
    scale = dq ** -0.5
    qb = q.reshape(B, nb, Q_BLOCK, Hk, G, dq).transpose(1, 0, 2, 3, 4, 5)

    def one(qblk):
        s = jnp.einsum('bqkgd,bskd->bkgqs', qblk, k).astype(jnp.float32) * scale
        p = jax.nn.softmax(s, axis=-1).astype(v.dtype)
        return jnp.einsum('bkgqs,bskd->bqkgd', p, v)

    o = lax.map(one, qb)
    return o.transpose(1, 0, 2, 3, 4, 5).reshape(B, S, H, dv)


def _neighbourhood_attention(q, k, v, rpb):
    B, S, H, d = q.shape
    rows = S // GRID_W
    kr = min(WIN_ROWS, rows)
    nk = kr * WIN_COLS
    t = jnp.arange(S)
    r = t // GRID_W
    c = t % GRID_W
    r0 = jnp.clip(r - kr // 2, 0, rows - kr)
    c0 = jnp.clip(c - WIN_COLS // 2, 0, GRID_W - WIN_COLS)
    key_r = r0[:, None, None] + jnp.arange(kr)[None, :, None]
    key_c = c0[:, None, None] + jnp.arange(WIN_COLS)[None, None, :]
    idx = (key_r * GRID_W + key_c).reshape(S, nk)
    dr = key_r - r[:, None, None] + (WIN_ROWS - 1)
    dc = key_c - c[:, None, None] + (WIN_COLS - 1)
    bias = rpb[:, dr, dc].reshape(H, S, nk)
    nb = S // Q_BLOCK
    scale = d ** -0.5
    qb = q.reshape(B, nb, Q_BLOCK, H, d).transpose(1, 0, 2, 3, 4)
    idxb = idx.reshape(nb, Q_BLOCK, nk)
    biasb = bias.reshape(H, nb, Q_BLOCK, nk).transpose(1, 0, 2, 3)

    def one(args):
        qblk, iblk, bblk = args
        kg = jnp.take(k, iblk, axis=1)
        vg = jnp.take(v, iblk, axis=1)
        s = jnp.einsum('bqhd,bqnhd->bhqn', qblk, kg).astype(jnp.float32) * scale
        s = s + bblk[None].astype(jnp.float32)
        p = jax.nn.softmax(s, axis=-1).astype(v.dtype)
        return jnp.einsum('bhqn,bqnhd->bqhd', p, vg)

    o = lax.map(one, (qb, idxb, biasb))
    return o.transpose(1, 0, 2, 3, 4).reshape(B, S, H, d)


def _layer(x, g_pre, g_post, w_in, a_qn, a_kn, b_qn, b_kvn, w_uq, w_ukv, rpb, w_out, row, col):
    B, S, _ = x.shape
    h = _rms_norm(x, g_pre)
    proj = jnp.einsum('bsd,de->bse', h, w_in)
    split_points = [int(p) for p in np.cumsum(IN_SPLITS)[:-1]]
    (a_q, a_k, a_v, a_g, b_cq, b_ckv, b_kr, b_g,
     c_q, c_k, c_v, c_g) = jnp.split(proj, split_points, axis=-1)

    qa = _rms_norm(a_q.reshape(B, S, A_HEADS, HEAD_DIM), a_qn)
    ka = _rms_norm(a_k.reshape(B, S, A_KV_HEADS, HEAD_DIM), a_kn)
    va = a_v.reshape(B, S, A_KV_HEADS, HEAD_DIM)
    qa = _rope_axial(qa, row, col)
    ka = _rope_axial(ka, row, col)
    o_a = _dense_block_attention(qa, ka, va).reshape(B, S, A_HEADS * HEAD_DIM) * jax.nn.silu(a_g)

    cq = _rms_norm(b_cq, b_qn)
    qb = jnp.einsum('bsr,re->bse', cq, w_uq).reshape(B, S, B_HEADS, B_NOPE + B_ROPE)
    q_nope, q_pe = qb[..., :B_NOPE], qb[..., B_NOPE:]
    q_pe = _rope_axial(q_pe, row, col)
    ckv = _rms_norm(b_ckv, b_kvn)
    kv = jnp.einsum('bsr,re->bse', ckv, w_ukv).reshape(B, S, B_HEADS, B_NOPE + B_V)
    k_nope, vb = kv[..., :B_NOPE], kv[..., B_NOPE:]
    k_pe = _rope_axial(b_kr.reshape(B, S, 1, B_ROPE), row, col)
    k_pe = jnp.broadcast_to(k_pe, (B, S, B_HEADS, B_ROPE))
    q_full = jnp.concatenate([q_nope, q_pe], axis=-1)
    k_full = jnp.concatenate([k_nope, k_pe], axis=-1)
    o_b = _dense_block_attention(q_full, k_full, vb).reshape(B, S, B_HEADS * B_V) * jax.nn.silu(b_g)

    qc = c_q.reshape(B, S, C_HEADS, HEAD_DIM)
    kc = c_k.reshape(B, S, C_HEADS, HEAD_DIM)
    vc = c_v.reshape(B, S, C_HEADS, HEAD_DIM)
    o_c = _neighbourhood_attention(qc, kc, vc, rpb).reshape(B, S, C_HEADS * HEAD_DIM) * jax.nn.silu(c_g)

    y = jnp.einsum('bse,ed->bsd', jnp.concatenate([o_a, o_b, o_c], axis=-1), w_out)
    return x + _rms_norm(y, g_post)


def setup_inputs(seed: int = 0) -> dict:
    key = jax.random.key(seed)
    ks = jax.random.split(key, 13)
    f32 = jnp.float32

    def gain(k, n):
        return 1.0 + 0.05 * jax.random.normal(k, (DEPTH, n), f32)

    x = jax.random.normal(ks[0], (BATCH, SEQ, D_MODEL), f32)
    norm_pre = gain(ks[1], D_MODEL)
    norm_post = gain(ks[2], D_MODEL)
    w_in = jax.random.normal(ks[3], (DEPTH, D_MODEL, D_IN), f32) * D_MODEL ** -0.5
    a_q_norm = gain(ks[4], HEAD_DIM)
    a_k_norm = gain(ks[5], HEAD_DIM)
    b_q_norm = gain(ks[6], B_Q_LORA)
    b_kv_norm = gain(ks[7], B_KV_LORA)
    b_w_uq = jax.random.normal(ks[8], (DEPTH, B_Q_LORA, B_HEADS * (B_NOPE + B_ROPE)), f32) * B_Q_LORA ** -0.5
    b_w_ukv = jax.random.normal(ks[9], (DEPTH, B_KV_LORA, B_HEADS * (B_NOPE + B_V)), f32) * B_KV_LORA ** -0.5
    c_rpb = 0.1 * jax.random.normal(ks[10], (DEPTH, C_HEADS, 2 * WIN_ROWS - 1, 2 * WIN_COLS - 1), f32)
    w_out = jax.random.normal(ks[11], (DEPTH, D_MIX, D_MODEL), f32) * D_MIX ** -0.5
    return {"x": x, "norm_pre": norm_pre, "norm_post": norm_post, "w_in": w_in,
            "a_q_norm": a_q_norm, "a_k_norm": a_k_norm, "b_q_norm": b_q_norm,
            "b_kv_norm": b_kv_norm, "b_w_uq": b_w_uq, "b_w_ukv": b_w_ukv,
            "c_rpb": c_rpb, "w_out": w_out}


def reference(x, norm_pre, norm_post, w_in, a_q_norm, a_k_norm, b_q_norm, b_kv_norm,
              b_w_uq, b_w_ukv, c_rpb, w_out):
    S = x.shape[1]
    t = jnp.arange(S)
    row = t // GRID_W
    col = t % GRID_W
    h = x
    for l in range(DEPTH):
        h = _layer(h, norm_pre[l], norm_post[l], w_in[l], a_q_norm[l], a_k_norm[l],
                   b_q_norm[l], b_kv_norm[l], b_w_uq[l], b_w_ukv[l], c_rpb[l], w_out[l],
                   row, col)
    return h
```

```python
import contextlib
import numpy as np
import ml_dtypes
import concourse.bass as bass
import concourse.mybir as mybir
from concourse.bass_utils import run_bass_kernel_spmd

F32 = mybir.dt.float32
BF16 = mybir.dt.bfloat16
U8 = mybir.dt.uint8
AF = mybir.ActivationFunctionType
ALU = mybir.AluOpType
ENGS = ('pe', 'act', 'dve', 'pool', 'sp')

S_TOK = 2048
D = 2048
DEPTH = 2
NBLK = 47
EPS = 1e-6
NEG = -80.0


class Buf:
    __slots__ = ('name', 'w', 'r', 'excl')

    def __init__(s, name, excl=False):
        s.name = name
        s.w = None
        s.r = {}
        s.excl = excl


class Sched:
    def __init__(s):
        s.q = {e: [] for e in ENGS}
        s.cnt = {e: 0 for e in ENGS}
        s.seen = {e: {} for e in ENGS}
        s.dcnt = {}

    def _waits(s, eng, reads, writes):
        need = {}

        def add(tok):
            if tok is None:
                return
            k, v = tok
            if need.get(k, 0) < v:
                need[k] = v
        for b in reads:
            add(b.w)
            if b.excl:
                for k, v in b.r.items():
                    if k != eng:
                        add((k, v))
        for b in writes:
            if b.w is not None and not (eng == 'pe' and b.w[0] == 'pe'):
                add(b.w)
            for k, v in b.r.items():
                if k != eng:
                    add((k, v))
        out = []
        for k, v in need.items():
            if s.seen[eng].get(k, 0) < v:
                s.seen[eng][k] = v
                out.append((k, v))
        return out

    def _mark(s, tok, reads, writes):
        for b in reads:
            if b.r.get(tok[0], 0) < tok[1]:
                b.r[tok[0]] = tok[1]
        for b in writes:
            b.w = tok
            b.r = {}

    def op(s, eng, fns, reads=(), writes=()):
        if not isinstance(fns, (list, tuple)):
            fns = [fns]
        waits = s._waits(eng, reads, writes)
        s.cnt[eng] += 1
        tok = (eng, s.cnt[eng])
        s.q[eng].append((waits, list(fns), (eng, 1)))
        s._mark(tok, reads, writes)
        return tok

    def dma(s, eng, fn, key, reads=(), writes=()):
        waits = s._waits(eng, reads, writes)
        s.dcnt[key] = s.dcnt.get(key, 0) + 16
        tok = (key, s.dcnt[key])
        s.q[eng].append((waits, [fn], (key, 16)))
        s._mark(tok, reads, writes)
        return tok

    def barrier(s):
        toks = [(e, s.cnt[e]) for e in ENGS if s.cnt[e] > 0] + list(s.dcnt.items())
        for e in ENGS:
            waits = []
            for k, v in toks:
                if k == e:
                    continue
                if s.seen[e].get(k, 0) < v:
                    s.seen[e][k] = v
                    waits.append((k, v))
            if waits:
                s.q[e].append((waits, [], None))

    def emit(s, nc):
        keys = list(ENGS) + list(s.dcnt.keys())
        with contextlib.ExitStack() as st:
            sems = {k: st.enter_context(nc.semaphore("s_" + str(k))) for k in keys}
            block = st.enter_context(nc.Block())

            def run(e, name):
                for waits, fns, inc in s.q[name]:
                    for k, v in waits:
                        e.wait_ge(sems[k], v)
                    ins = None
                    for fn in fns:
                        ins = fn(e)
                    if inc is not None and ins is not None:
                        ins.then_inc(sems[inc[0]], inc[1])

            @block.tensor
            def _(e):
                run(e, 'pe')

            @block.scalar
            def _(e):
                run(e, 'act')

            @block.vector
            def _(e):
                run(e, 'dve')

            @block.gpsimd
            def _(e):
                run(e, 'pool')

            @block.sync
            def _(e):
                run(e, 'sp')


def _block_cols():
    blks = []
    for i in range(4):
        blks.append(list(range(2560 + 128 * i, 2560 + 128 * (i + 1))))
    for i in range(2):
        blks.append(list(range(3072 + 128 * i, 3072 + 128 * (i + 1))))
    blks.append(list(range(3328, 3392)) + list(range(3328, 3392)))
    for h in range(4):
        blks.append(list(range(3392 + 128 * h, 3392 + 128 * (h + 1))))
    for h in range(4):
        for base in (3904, 4416, 4928, 5440):
            blks.append(list(range(base + 128 * h, base + 128 * (h + 1))))
    for g in range(2):
        blks.append(list(range(1024 + 128 * g, 1024 + 128 * (g + 1))))
        blks.append(list(range(1280 + 128 * g, 1280 + 128 * (g + 1))))
        for h in range(4 * g, 4 * g + 4):
            blks.append(list(range(128 * h, 128 * (h + 1))))
            blks.append(list(range(1536 + 128 * h, 1536 + 128 * (h + 1))))
    assert len(blks) == NBLK
    return np.array(blks, dtype=np.int64)


def _c_slots():
    slots = []
    for u in range(16):
        if 2 <= u <= 13:
            slots.append([(u + d, d + 2) for d in range(-2, 3)])
        else:
            t0 = 0 if u < 2 else 12
            sp = {0: 0, 1: 1, 14: 2, 15: 3}[u]
            slots.append([(t0 + j, 5 + 4 * sp + j) for j in range(4)])
    return slots


def _c_bias_index():
    ent = []
    for d in range(-2, 3):
        ent.append((6, 6 + d))
    for u in (0, 1, 14, 15):
        t0 = 0 if u < 2 else 12
        for j in range(4):
            ent.append((u, t0 + j))
    dr = np.zeros((21, 128, 128), np.int64)
    dc = np.zeros((21, 128, 128), np.int64)
    va = np.zeros((21, 128, 128), bool)
    kl = np.arange(128)
    for e, (u, t) in enumerate(ent):
        q = u * 128 + kl
        k = t * 128 + kl
        qr, qc = q // 64, q % 64
        kr, kc = k // 64, k % 64
        r0 = np.clip(qr - 4, 0, 24)
        c0 = np.clip(qc - 8, 0, 48)
        ddr = kr[:, None] - qr[None, :]
        ddc = kc[:, None] - qc[None, :]
        ok = ((kr[:, None] >= r0[None, :]) & (kr[:, None] < r0[None, :] + 8) &
              (kc[:, None] >= c0[None, :]) & (kc[:, None] < c0[None, :] + 16))
        va[e] = ok
        dr[e] = np.where(ok, ddr + 7, 0)
        dc[e] = np.where(ok, ddc + 15, 0)
    return dr, dc, va


def _const_tables():
    ident = np.eye(128, dtype=np.float32)
    ones = np.ones((128, 128), np.float32)
    permA = np.zeros((128, 128), np.float32)
    for m in range(128):
        p = m + 32 if (m % 64) < 32 else m - 32
        permA[p, m] = 1.0
    permB = np.zeros((128, 128), np.float32)
    for m in range(64):
        p = m + 16 if (m % 32) < 16 else m - 16
        permB[p, m] = 1.0
    cmats = np.concatenate([ident, ones, permA, permB], axis=1).astype(ml_dtypes.bfloat16)
    theta = np.float32(10000.0)
    tabs = np.zeros((128, 4 * 64), np.float32)
    invA = (1.0 / (theta ** (np.arange(0, 64, 2, dtype=np.float32) / np.float32(64)))).astype(np.float32)
    invB = (1.0 / (theta ** (np.arange(0, 32, 2, dtype=np.float32) / np.float32(32)))).astype(np.float32)
    pos = np.arange(64, dtype=np.float32)
    for d in range(128):
        i = d % 32
        ang = (pos * invA[i]).astype(np.float32)
        sign = -1.0 if (d % 64) < 32 else 1.0
        tabs[d, 0:64] = np.cos(ang)
        tabs[d, 64:128] = sign * np.sin(ang)
    for d in range(64):
        i = d % 16
        ang = (pos * invB[i]).astype(np.float32)
        sign = -1.0 if (d % 32) < 16 else 1.0
        tabs[d, 128:192] = np.cos(ang)
        tabs[d, 192:256] = sign * np.sin(ang)
    _, _, va = _c_bias_index()
    cmask = np.where(va, 0.0, NEG).astype(np.float32)
    cmask = np.ascontiguousarray(cmask.transpose(1, 0, 2)).reshape(128, 21 * 128).astype(ml_dtypes.bfloat16)
    return cmats, tabs, cmask


def build(nc, L=DEPTH, dbg=None):
    dbg = dbg or {}
    stop_after = dbg.get('stop_after')

    def dram(name, shape, dt, kind):
        return nc.dram_tensor(name, shape, dt, kind=kind).ap()
    x_d = dram("x", [S_TOK, D], F32, "ExternalInput")
    win_d = dram("win", [L * NBLK * 128, 2048], F32, "ExternalInput")
    wuq_d = dram("wuq", [L * 128, 4 * 768], F32, "ExternalInput")
    wukv_d = dram("wukv", [L * 128, 2 * 1024], F32, "ExternalInput")
    wout_d = dram("wout", [L * 16 * 128, 2048], F32, "ExternalInput")
    gpre_d = dram("gpre", [L, D], F32, "ExternalInput")
    gpost_d = dram("gpost", [L, D], F32, "ExternalInput")
    vecs_d = dram("vecs", [128, L * 8], F32, "ExternalInput")
    cbias_d = dram("cbias", [L * 4 * 128, 21 * 128], F32, "ExternalInput")
    cmask_d = dram("cmask", [128, 21 * 128], BF16, "ExternalInput")
    cmats_d = dram("cmats", [128, 4 * 128], BF16, "ExternalInput")
    ctabs_d = dram("ctabs", [128, 4 * 64], F32, "ExternalInput")
    out_d = dram("out", [S_TOK, D], F32, "ExternalOutput")
    dbg_d = {}
    for name, shape in dbg.get('outs', {}).items():
        dbg_d[name] = dram(name, shape, F32, "ExternalOutput")

    S = Sched()
    st = contextlib.ExitStack()
    ARENA = 211968
    arena = st.enter_context(nc.sbuf_tensor("arena", [128, ARENA], U8))
    psum = st.enter_context(nc.psum_tensor("psum", [128, 4096], F32))
    psum_bf = psum[:, :].bitcast(BF16)

    def carve(off, nbytes, dt):
        assert off + nbytes <= ARENA, (off, nbytes)
        return arena[:, off:off + nbytes].bitcast(dt)

    bufs2 = [carve(0, 65536, BF16), carve(65536, 65536, BF16)]
    off = 131072
    cm = carve(off, 1024, BF16); off += 1024
    ident, ones, permA, permB = (cm[:, i * 128:(i + 1) * 128] for i in range(4))
    tabs = carve(off, 1024, F32); off += 1024
    vecs = carve(off, 64, F32); off += 64
    stat = carve(off, 256, F32); off += 256
    off = 134144
    NW = 3
    wsl = [carve(off + i * 4096, 4096, BF16) for i in range(NW)]; off += NW * 4096
    OV = off
    kT = carve(off, 4096, BF16); off += 4096
    Vt = carve(off, 4096, BF16); off += 4096
    qT = [carve(off + i * 4096, 4096, BF16) for i in range(2)]; off += 8192
    gT = [carve(off + i * 4096, 4096, BF16) for i in range(2)]; off += 8192
    Psb = [carve(off + i * 2048, 2048, BF16) for i in range(2)]; off += 4096
    sqs = [carve(off + i * 1024, 1024, BF16) for i in range(2)]; off += 2048
    qgs = [carve(off + i * 1024, 1024, BF16) for i in range(2)]; off += 2048
    rsb = [carve(off + i * 2048, 2048, F32) for i in range(2)]; off += 4096
    t1s = [carve(off + i * 2048, 2048, F32) for i in range(2)]; off += 4096
    t2s = [carve(off + i * 2048, 2048, F32) for i in range(2)]; off += 4096
    rden = carve(off, 2048, F32); off += 2048
    otmp = carve(off, 2048, F32); off += 2048
    cmask = carve(off, 5376, BF16); off += 5376
    biasm = [carve(off + i * 5376, 5376, BF16) for i in range(2)]; off += 10752
    assert off <= ARENA, off
    off = OV
    gpre_bc = carve(off, 8192, F32); off += 8192
    gpost_bc = carve(off, 8192, F32); off += 8192
    xt = [carve(off + i * 8192, 8192, F32) for i in range(2)]; off += 16384
    yraw = [carve(off + i * 8192, 8192, F32) for i in range(2)]; off += 16384
    xn = [carve(off + i * 4096, 4096, BF16) for i in range(2)]; off += 8192
    junk = carve(off, 1024, BF16); off += 1024
    assert off <= ARENA

    def bank(i, n=1):
        return psum[:, i * 512:(i + n) * 512]
    Bbank = [Buf("bank%d" % i, excl=True) for i in range(8)]

    Bconst = Buf("const")
    BhT = [[Buf("hT%d_%d" % (p, t)) for t in range(16)] for p in range(2)]
    BoT = [[[Buf("oT%d_%d_%d" % (p, m, c)) for c in range(4)] for m in range(16)] for p in range(2)]
    Bw = [Buf("w%d" % i) for i in range(NW)]
    BkT = [Buf("kT%d" % c) for c in range(4)]
    BVt = [Buf("Vt%d" % c) for c in range(4)]
    BqT = [[Buf("qT%d_%d" % (i, c)) for c in range(4)] for i in range(2)]
    BgT = [[Buf("gT%d_%d" % (i, c)) for c in range(4)] for i in range(2)]
    BP = [Buf("P%d" % i) for i in range(2)]
    Bsq = [Buf("sq%d" % i) for i in range(2)]
    Bqg = [Buf("qg%d" % i) for i in range(2)]
    Brs = [Buf("rs%d" % i) for i in range(2)]
    Bt1 = [Buf("t1%d" % i) for i in range(2)]
    Bt2 = [Buf("t2%d" % i) for i in range(2)]
    Brden = Buf("rden"); Botmp = Buf("otmp")
    Bcmask = Buf("cmask"); Bbiasm = [Buf("biasm%d" % i) for i in range(2)]
    Bgpre = Buf("gpre"); Bgpost = Buf("gpost")
    Bxt = [Buf("xt%d" % i) for i in range(2)]
    Byraw = [Buf("yraw%d" % i) for i in range(2)]
    Bxn = [Buf("xn%d" % i) for i in range(2)]
    Bst_h = [Buf("sth%d" % i) for i in range(2)]
    Bst_y = [Buf("sty%d" % i) for i in range(2)]
    Bout = [Buf("outrow%d" % t) for t in range(16)]

    S.dma('sp', lambda e: e.dma_start(out=cm, in_=cmats_d), 'c0', writes=[Bconst])
    S.dma('sp', lambda e: e.dma_start(out=tabs, in_=ctabs_d), 'c1', writes=[Bconst])
    S.dma('sp', lambda e: e.dma_start(out=vecs[:, 0:L * 8], in_=vecs_d), 'c2', writes=[Bconst])
    TAc, TAs, TBc, TBs = (tabs[:, i * 64:(i + 1) * 64] for i in range(4))
    S.op('pool', lambda e: e.memset(stat[:, 63:64], -1.0), reads=[], writes=[Bconst])

    def v3(ap, k):
        return ap.rearrange("p (k n) -> p k n", k=k)

    wstate = {'issued': 0}

    def w_issue_upto(gi):
        while wstate['issued'] <= gi and wstate['issued'] < L * NBLK:
            i = wstate['issued']
            sl = i % NW
            S.dma('pool', lambda e, i=i, sl=sl: e.dma_start(out=wsl[sl], in_=win_d[i * 128:(i + 1) * 128, :]),
                  'w%d' % sl, writes=[Bw[sl]])
            wstate['issued'] += 1

    def wget(gi):
        w_issue_upto(gi + NW - 1)
        sl = gi % NW
        return v3(wsl[sl], 16), Bw[sl]

    pp_state = {'i': 0}

    def next_pp():
        i = pp_state['i']
        pp_state['i'] ^= 1
        return 6 + i

    def proj_fm(hT, BhTl, w3, Bwb, M, tc, bk):
        h3 = v3(hT, 16)
        fns = []
        for kc in range(16):
            fns.append(lambda e, kc=kc: e.matmul(bank(bk)[0:M, :], w3[:, kc, 0:M], h3[:, kc, tc * 512:(tc + 1) * 512],
                                                 start=(kc == 0), stop=(kc == 15)))
        S.op('pe', fns, reads=[Bwb] + BhTl[tc * 4:(tc + 1) * 4], writes=[Bbank[bk]])

    def rstd_from_bank(bk, P, r_ap, Br, n, reads_extra=()):
        S.op('act', lambda e: e.activation(out=r_ap, in_=bank(bk)[0:P, :], func=AF.Ln, scale=1.0 / n, bias=EPS),
             reads=[Bbank[bk]] + list(reads_extra), writes=[Br])
        S.op('act', lambda e: e.activation(out=r_ap, in_=r_ap, func=AF.Exp, scale=-0.5), reads=[Br], writes=[Br])

    def rope_chunk(src_bk, P, half, tc, cosT, sinT, perm, g_ap, norm_n, dst_ap, Bdst, i2, scale_mult=None):
        sq, qg, r, t1, t2 = sqs[i2][0:P, :], qgs[i2][0:P, :], rsb[i2][0:P, :], t1s[i2][0:P, :], t2s[i2][0:P, :]
        ss_bk, sw_bk = (0, 2) if i2 == 0 else (1, 3)
        src = bank(src_bk)[0:P, :]
        if norm_n:
            S.op('act', lambda e: e.activation(out=sq, in_=src, func=AF.Square), reads=[Bbank[src_bk]], writes=[Bsq[i2]])
        if g_ap is not None:
            S.op('dve', lambda e: e.tensor_scalar(out=qg, in0=src, scalar1=g_ap, scalar2=None, op0=ALU.mult),
                 reads=[Bbank[src_bk], Bconst], writes=[Bqg[i2]])
        else:
            S.op('dve', lambda e: e.tensor_copy(out=qg, in_=src), reads=[Bbank[src_bk]], writes=[Bqg[i2]])
        if norm_n:
            S.op('pe', lambda e: e.matmul(bank(ss_bk)[0:P, :], ones[0:P, 0:P], sq, start=True, stop=True),
                 reads=[Bsq[i2], Bconst], writes=[Bbank[ss_bk]])
        S.op('pe', lambda e: e.matmul(bank(sw_bk)[0:P, :], perm[0:P, 0:P], qg, start=True, stop=True),
             reads=[Bqg[i2], Bconst], writes=[Bbank[sw_bk]])
        if norm_n:
            S.op('act', lambda e: e.activation(out=r, in_=bank(ss_bk)[0:P, :], func=AF.Ln, scale=1.0, bias=norm_n * EPS),
                 reads=[Bbank[ss_bk]], writes=[Brs[i2]])
            S.op('act', lambda e: e.activation(out=r, in_=r, func=AF.Exp, scale=-0.5), reads=[Brs[i2]], writes=[Brs[i2]])
        sw = bank(sw_bk)
        lo, hi = slice(0, half), slice(half, P)

        def rowb(tab):
            return tab[lo, tc * 8:(tc + 1) * 8].unsqueeze(2).broadcast_to([half, 8, 64])

        def colb(tab):
            return tab[hi, 0:64].unsqueeze(1).broadcast_to([P - half, 8, 64])

        def v8(ap):
            return ap.rearrange("p (a b) -> p a b", a=8)
        S.op('pool', [lambda e: e.tensor_tensor(out=v8(t1s[i2][lo, :]), in0=v8(qgs[i2][lo, :]), in1=rowb(cosT), op=ALU.mult),
                      lambda e: e.tensor_tensor(out=v8(t1s[i2][hi, :]), in0=v8(qgs[i2][hi, :]), in1=colb(cosT), op=ALU.mult)],
             reads=[Bqg[i2], Bconst], writes=[Bt1[i2]])
        S.op('dve', [lambda e: e.tensor_tensor(out=v8(t2s[i2][lo, :]), in0=v8(sw[lo, :]), in1=rowb(sinT), op=ALU.mult),
                     lambda e: e.tensor_tensor(out=v8(t2s[i2][hi, :]), in0=v8(sw[hi, :]), in1=colb(sinT), op=ALU.mult)],
             reads=[Bbank[sw_bk], Bconst], writes=[Bt2[i2]])
        if norm_n:
            S.op('dve', lambda e: e.tensor_tensor(out=t1, in0=t1, in1=t2, op=ALU.add), reads=[Bt1[i2], Bt2[i2]], writes=[Bt1[i2]])
            S.op('dve', lambda e: e.tensor_tensor(out=dst_ap, in0=t1, in1=r, op=ALU.mult),
                 reads=[Bt1[i2], Brs[i2]], writes=[Bdst])
        else:
            S.op('dve', lambda e: e.tensor_tensor(out=dst_ap, in0=t1, in1=t2, op=ALU.add), reads=[Bt1[i2], Bt2[i2]], writes=[Bdst])

    def silu_chunk(src_bk, dst_ap, Bdst, i2):
        S.op('act', lambda e: e.activation(out=dst_ap, in_=bank(src_bk), func=AF.Silu), reads=[Bbank[src_bk]], writes=[Bdst])

    pend = {'f': None}

    def flush_pend():
        f = pend['f']
        pend['f'] = None
        if f is not None:
            f()

    def proj_block_fm(hT, BhTl, gi, M, consume):
        w3, Bwb = wget(gi)
        for tc in range(4):
            bk = next_pp()
            proj_fm(hT, BhTl, w3, Bwb, M, tc, bk)
            flush_pend()
            pend['f'] = (lambda tc=tc, bk=bk: consume(tc, bk))

    def proj_block_tm(hT, BhTl, gi, dst, Bdstl):
        w3, Bwb = wget(gi)
        h3 = v3(hT, 16)
        d3 = v3(dst, 16)
        for tq in range(4):
            bk = next_pp()
            fns = []
            for j in range(4):
                tt = tq * 4 + j
                for kc in range(16):
                    fns.append(lambda e, kc=kc, tt=tt, j=j, bk=bk: e.matmul(bank(bk)[:, j * 128:(j + 1) * 128], h3[:, kc, tt * 128:(tt + 1) * 128],
                                                                      w3[:, kc, :], start=(kc == 0), stop=(kc == 15)))
            S.op('pe', fns, reads=[Bwb] + BhTl[tq * 4:(tq + 1) * 4], writes=[Bbank[bk]])
            flush_pend()
            S.op('act', lambda e, tq=tq, bk=bk: e.activation(out=dst[:, tq * 512:(tq + 1) * 512], in_=bank(bk), func=AF.Copy),
                 reads=[Bbank[bk]], writes=[Bdstl[tq]])

    ep_pend = {'f': None}

    def flush_ep():
        f = ep_pend['f']
        ep_pend['f'] = None
        if f is not None:
            f()

    def attn_epilogue(oTbuf, BoTl, mc, qc, gslot, ob, db):
        o3 = v3(oTbuf, 16)
        S.op('act', lambda e: e.activation(out=rden, in_=bank(db), func=AF.Ln), reads=[Bbank[db]], writes=[Brden])
        S.op('act', lambda e: e.activation(out=rden, in_=rden, func=AF.Exp, scale=-1.0), reads=[Brden], writes=[Brden])
        S.op('dve', lambda e: e.tensor_tensor(out=otmp, in0=bank(ob), in1=rden, op=ALU.mult), reads=[Bbank[ob], Brden], writes=[Botmp])
        S.op('pool', lambda e: e.tensor_tensor(out=o3[:, mc, qc * 512:(qc + 1) * 512], in0=otmp, in1=gT[gslot][:, qc * 512:(qc + 1) * 512], op=ALU.mult),
             reads=[Botmp, BgT[gslot][qc]], writes=[BoTl[mc][qc]])

    def attn_dense(oTbuf, BoTl, mc, qslot, gslot, scale, kpe=None, qpe=None, Bkpe=None, Bqpe=None):
        q_ap = qT[qslot]
        seq = [(qc, kp) for qc in range(4) for kp in range(8)]

        def s_mm(idx, extra_reads=()):
            qc, kp = seq[idx]
            sb = idx % 2
            fns = []
            for j in range(2):
                kt = 2 * kp + j
                o_ = bank(2 * sb + j)
                if kpe is None:
                    fns.append(lambda e, o_=o_, kt=kt, qc=qc: e.matmul(o_, kT[:, kt * 128:(kt + 1) * 128], q_ap[:, qc * 512:(qc + 1) * 512], start=True, stop=True))
                else:
                    fns.append(lambda e, o_=o_, kt=kt, qc=qc: e.matmul(o_, kT[:, kt * 128:(kt + 1) * 128], q_ap[:, qc * 512:(qc + 1) * 512], start=True, stop=False))
                    fns.append(lambda e, o_=o_, kt=kt, qc=qc: e.matmul(o_, kpe[0:64, kt * 128:(kt + 1) * 128], qpe[0:64, qc * 512:(qc + 1) * 512], start=False, stop=True))
            rd = [BkT[kp // 2], BqT[qslot][qc]] + list(extra_reads)
            if kpe is not None:
                rd += [Bkpe, Bqpe[qc]]
            S.op('pe', fns, reads=rd, writes=[Bbank[2 * sb], Bbank[2 * sb + 1]])

        def act_part(idx):
            qc, kp = seq[idx]
            sb = idx % 2
            S.op('act', lambda e: e.activation(out=Psb[sb], in_=bank(2 * sb, 2), func=AF.Exp, scale=scale),
                 reads=[Bbank[2 * sb], Bbank[2 * sb + 1]], writes=[BP[sb]])
            if kp == 2:
                flush_ep()
            S.op('dve', lambda e: e.tensor_tensor(out=sqs[sb], in0=Psb[sb][:, 0:512], in1=Psb[sb][:, 512:1024], op=ALU.add),
                 reads=[BP[sb]], writes=[Bsq[sb]])

        def pe_part(idx):
            qc, kp = seq[idx]
            sb = idx % 2
            ob, db = (4, 5) if qc % 2 == 0 else (6, 7)
            fns = []
            for j in range(2):
                kt = 2 * kp + j
                first = (kp == 0 and j == 0)
                last = (kp == 7 and j == 1)
                fns.append(lambda e, kt=kt, j=j, first=first, last=last: e.matmul(bank(ob), Vt[:, kt * 128:(kt + 1) * 128], Psb[sb][:, j * 512:(j + 1) * 512], start=first, stop=last))
            rd = [BP[sb], BVt[kp // 2], Bconst]
            wr = [Bbank[ob]]
            if kp > 0:
                fns.append(lambda e, kp=kp: e.matmul(bank(db), ones, sqs[1 - sb], start=(kp == 1), stop=False))
                rd.append(Bsq[1 - sb])
                wr.append(Bbank[db])
            S.op('pe', fns, reads=rd, writes=wr)
            if kp == 7:
                S.op('pe', lambda e: e.matmul(bank(db), ones, sqs[sb], start=False, stop=True), reads=[Bsq[sb], Bconst], writes=[Bbank[db]])
                ep_pend['f'] = (lambda qc=qc, ob=ob, db=db: attn_epilogue(oTbuf, BoTl, mc, qc, gslot, ob, db))
        s_mm(0)
        flush_pend()
        s_mm(1)
        for idx in range(len(seq)):
            act_part(idx)
            if idx + 2 < len(seq):
                s_mm(idx + 2, extra_reads=[Bsq[1 - idx % 2], BP[idx % 2]] if seq[idx][1] > 0 else [BP[idx % 2]])
            pe_part(idx)
        flush_ep()

    slots_c = _c_slots()

    def attn_window(oTbuf, BoTl, mc, qslot, gslot, bslot):
        q_ap = qT[qslot]
        bm = biasm[bslot]

        def s_mm(u):
            sb = u % 2
            fns = []
            for s_, (t, eidx) in enumerate(slots_c[u]):
                o_ = psum[:, sb * 1024 + s_ * 128: sb * 1024 + (s_ + 1) * 128]
                fns.append(lambda e, o_=o_, t=t, u=u: e.matmul(o_, kT[:, t * 128:(t + 1) * 128], q_ap[:, u * 128:(u + 1) * 128], start=True, stop=False))
                fns.append(lambda e, o_=o_, eidx=eidx: e.matmul(o_, ident, bm[:, eidx * 128:(eidx + 1) * 128], start=False, stop=True))
            ts = sorted(set(t // 4 for t, _ in slots_c[u]))
            S.op('pe', fns, reads=[BkT[c] for c in ts] + [BqT[qslot][u // 4], Bbiasm[bslot], Bconst], writes=[Bbank[2 * sb], Bbank[2 * sb + 1]])

        def rest(u):
            sb = u % 2
            ns = len(slots_c[u])
            S.op('act', lambda e: e.activation(out=Psb[sb][:, 0:ns * 128], in_=psum[:, sb * 1024: sb * 1024 + ns * 128], func=AF.Exp, scale=1.0),
                 reads=[Bbank[2 * sb], Bbank[2 * sb + 1]], writes=[BP[sb]])
            uu = u % 4
            ob, db = (4, 5) if (u // 4) % 2 == 0 else (6, 7)
            fns = []
            for s_, (t, eidx) in enumerate(slots_c[u]):
                first = (s_ == 0)
                last = (s_ == ns - 1)
                fns.append(lambda e, t=t, s_=s_, first=first, last=last: e.matmul(bank(ob)[:, uu * 128:(uu + 1) * 128], Vt[:, t * 128:(t + 1) * 128], Psb[sb][:, s_ * 128:(s_ + 1) * 128], start=first, stop=last))
                fns.append(lambda e, s_=s_, first=first, last=last: e.matmul(bank(db)[:, uu * 128:(uu + 1) * 128], ones, Psb[sb][:, s_ * 128:(s_ + 1) * 128], start=first, stop=last))
            ts = sorted(set(t // 4 for t, _ in slots_c[u]))
            S.op('pe', fns, reads=[BP[sb], Bconst] + [BVt[c] for c in ts], writes=[Bbank[ob], Bbank[db]])
            if uu == 3:
                attn_epilogue(oTbuf, BoTl, mc, u // 4, gslot, ob, db)
        s_mm(0)
        flush_pend()
        for u in range(16):
            if u + 1 < 16:
                s_mm(u + 1)
            rest(u)

    def hT_part1(src_ap, Bsrc, i2, scol):
        ss = stat[:, scol:scol + 1]
        rr = stat[:, scol + 1:scol + 2]
        S.op('act', lambda e: e.activation(out=xn[i2], in_=src_ap, func=AF.Square, accum_out=ss), reads=[Bsrc], writes=[Bxn[i2], Bst_h[i2]])
        S.op('act', lambda e: e.activation(out=rr, in_=ss, func=AF.Ln, scale=1.0 / D, bias=EPS), reads=[Bst_h[i2]], writes=[Bst_h[i2]])
        S.op('act', lambda e: e.activation(out=rr, in_=rr, func=AF.Exp, scale=-0.5), reads=[Bst_h[i2]], writes=[Bst_h[i2]])
        S.op('dve', lambda e: e.scalar_tensor_tensor(out=xn[i2], in0=src_ap, scalar=rr, in1=gpre_bc, op0=ALU.mult, op1=ALU.mult),
             reads=[Bsrc, Bst_h[i2], Bgpre], writes=[Bxn[i2]])

    def hT_part2(tt, hTdst, Bdst_list, i2):
        pst = psum_bf[:, 6 * 1024: 8 * 1024]
        fns = [lambda e, kc=kc: e.transpose(pst[:, kc * 128:(kc + 1) * 128], xn[i2][:, kc * 128:(kc + 1) * 128], ident) for kc in range(16)]
        S.op('pe', fns, reads=[Bxn[i2], Bconst], writes=[Bbank[6], Bbank[7]])
        h3 = v3(hTdst, 16)
        S.op('dve', lambda e: e.tensor_copy(out=h3[:, :, tt * 128:(tt + 1) * 128], in_=pst.rearrange("p (k n) -> p k n", k=16)),
             reads=[Bbank[6], Bbank[7]], writes=Bdst_list)

    def make_hT_tile(src_ap, Bsrc, tt, hTdst, Bdst_list, li, i2, scol):
        hT_part1(src_ap, Bsrc, i2, scol)
        hT_part2(tt, hTdst, Bdst_list, i2)

    def load_gains(li, with_post):
        S.dma('sp', lambda e: e.dma_start(out=gpre_bc, in_=gpre_d[li:li + 1, :].broadcast_to([128, D])), 'gpre', writes=[Bgpre])
        if with_post:
            S.dma('sp', lambda e: e.dma_start(out=gpost_bc, in_=gpost_d[li - 1:li, :].broadcast_to([128, D])), 'gpost', writes=[Bgpost])

    hp = 0
    load_gains(0, False)
    w_issue_upto(NW - 1)
    xb4 = [xt[0], xt[1], yraw[0], yraw[1]]
    Bxb4 = [Bxt[0], Bxt[1], Byraw[0], Byraw[1]]
    xkeys = ['xt0', 'xt1', 'xp2', 'xp3']
    for tt in range(16):
        i2 = tt % 2
        i4 = tt % 4
        S.dma('sp', lambda e, tt=tt, i4=i4: e.dma_start(out=xb4[i4], in_=x_d[tt * 128:(tt + 1) * 128, :]), xkeys[i4], writes=[Bxb4[i4]])
        hT_part1(xb4[i4], Bxb4[i4], i2, 4 * i2)
        if tt > 0:
            hT_part2(tt - 1, bufs2[hp], [BhT[hp][tt - 1]], (tt - 1) % 2)
    hT_part2(15, bufs2[hp], [BhT[hp][15]], 1)

    if 'hT' in dbg_d:
        S.barrier()
        for kc in range(16):
            S.op('dve', lambda e, kc=kc: e.tensor_copy(out=yraw[0], in_=v3(bufs2[hp], 16)[:, kc, :]), reads=BhT[hp], writes=[Byraw[0]])
            S.dma('sp', lambda e, kc=kc: e.dma_start(out=dbg_d['hT'][kc * 128:(kc + 1) * 128, :], in_=yraw[0]), 'dbg', reads=[Byraw[0]])

    def do_layer(li, hp):
        if stop_after == 'pre':
            return False
        hT = bufs2[hp]
        oT = bufs2[1 - hp]
        BhTl = BhT[hp]
        BoTl = BoT[1 - hp]
        vb = li * 8
        S.barrier()
        gbase = li * NBLK

        o3 = v3(oT, 16)
        cqn = oT[:, 12 * 2048:16 * 2048]
        ckvn = oT[:, 0:2 * 2048]
        kpe = oT[:, 2 * 2048:3 * 2048]
        qpe = [oT[:, 3 * 2048:4 * 2048], oT[:, 4 * 2048:5 * 2048]]
        wuq_sb = oT[:, 5 * 2048:5 * 2048 + 3072]
        wukv_sb = oT[:, 7 * 2048:8 * 2048]
        Bcqn = [Buf("cqn%d" % c) for c in range(4)]
        Bckvn = [Buf("ckvn%d" % c) for c in range(4)]
        Bkpe = Buf("kpe")
        Bqpe = [[Buf("qpe%d_%d" % (i, c)) for c in range(4)] for i in range(2)]
        Bwu = Buf("wu")
        alias_all = [BoTl[m][c] for m in list(range(0, 8)) + list(range(12, 16)) for c in range(4)]
        S.dma('pool', lambda e: e.dma_start(out=wuq_sb, in_=wuq_d[li * 128:(li + 1) * 128, :]), 'wu', writes=[Bwu] + alias_all)
        S.dma('pool', lambda e: e.dma_start(out=wukv_sb, in_=wukv_d[li * 128:(li + 1) * 128, :]), 'wu', writes=[Bwu])
        wuq3 = v3(wuq_sb, 4)
        wukv3 = v3(wukv_sb, 2)

        def lowrank_norm(gi0, nblk, dstbuf, Bdst, gcol, n):
            d3 = v3(dstbuf, nblk)
            for blk in range(nblk):
                def consume(tc, bk, blk=blk):
                    i2 = tc % 2
                    S.op('act', lambda e: e.activation(out=sqs[i2], in_=bank(bk), func=AF.Square), reads=[Bbank[bk]], writes=[Bsq[i2]])
                    S.op('dve', lambda e: e.tensor_copy(out=d3[:, blk, tc * 512:(tc + 1) * 512], in_=bank(bk)), reads=[Bbank[bk]], writes=[Bdst[tc]])
                    S.op('pe', lambda e: e.matmul(bank(tc), ones, sqs[i2], start=(blk == 0), stop=(blk == nblk - 1)),
                         reads=[Bsq[i2], Bconst], writes=[Bbank[tc]])
                proj_block_fm(hT, BhTl, gi0 + blk, 128, consume)
            flush_pend()
            for tc in range(4):
                i2 = tc % 2
                rstd_from_bank(tc, 128, rsb[i2], Brs[i2], float(n))
                for blk in range(nblk):
                    S.op('dve', lambda e, blk=blk, tc=tc, i2=i2: e.scalar_tensor_tensor(
                        out=d3[:, blk, tc * 512:(tc + 1) * 512], in0=d3[:, blk, tc * 512:(tc + 1) * 512],
                        scalar=vecs[:, vb + gcol + blk: vb + gcol + blk + 1], in1=rsb[i2], op0=ALU.mult, op1=ALU.mult),
                        reads=[Bdst[tc], Brs[i2], Bconst], writes=[Bdst[tc]])

        lowrank_norm(gbase + 0, 4, cqn, Bcqn, 2, 512)
        if stop_after == 'B1':
            return False
        lowrank_norm(gbase + 4, 2, ckvn, Bckvn, 6, 256)
        proj_block_fm(hT, BhTl, gbase + 6, 64,
                      lambda tc, bk: rope_chunk(bk, 64, 32, tc, TBc, TBs, permB, None, 0, kpe[0:64, tc * 512:(tc + 1) * 512], Bkpe, tc % 2))

        def prepB(h, slot):
            proj_block_fm(hT, BhTl, gbase + 7 + h, 128,
                          lambda tc, bk: silu_chunk(bk, gT[slot][:, tc * 512:(tc + 1) * 512], BgT[slot][tc], tc % 2))
            for tc in range(4):
                bk = next_pp()
                fns = [lambda e, kc=kc, bk=bk, tc=tc: e.matmul(bank(bk), wuq3[:, kc, h * 192:h * 192 + 128], v3(cqn, 4)[:, kc, tc * 512:(tc + 1) * 512],
                                                               start=(kc == 0), stop=(kc == 3)) for kc in range(4)]
                S.op('pe', fns, reads=[Bwu, Bcqn[tc]], writes=[Bbank[bk]])
                flush_pend()
                S.op('act', lambda e, bk=bk, tc=tc: e.activation(out=qT[slot][:, tc * 512:(tc + 1) * 512], in_=bank(bk), func=AF.Copy),
                     reads=[Bbank[bk]], writes=[BqT[slot][tc]])
                bk = next_pp()
                fns = [lambda e, kc=kc, bk=bk, tc=tc: e.matmul(bank(bk)[0:64, :], wuq3[:, kc, h * 192 + 128:h * 192 + 192], v3(cqn, 4)[:, kc, tc * 512:(tc + 1) * 512],
                                                               start=(kc == 0), stop=(kc == 3)) for kc in range(4)]
                S.op('pe', fns, reads=[Bwu, Bcqn[tc]], writes=[Bbank[bk]])
                rope_chunk(bk, 64, 32, tc, TBc, TBs, permB, None, 0, qpe[slot][0:64, tc * 512:(tc + 1) * 512], Bqpe[slot][tc], tc % 2)

        def prepB_kv(h):
            for tc in range(4):
                bk = next_pp()
                fns = [lambda e, kc=kc, bk=bk, tc=tc: e.matmul(bank(bk), wukv3[:, kc, h * 256:h * 256 + 128], v3(ckvn, 2)[:, kc, tc * 512:(tc + 1) * 512],
                                                               start=(kc == 0), stop=(kc == 1)) for kc in range(2)]
                S.op('pe', fns, reads=[Bwu, Bckvn[tc]], writes=[Bbank[bk]])
                flush_pend()
                S.op('act', lambda e, bk=bk, tc=tc: e.activation(out=kT[:, tc * 512:(tc + 1) * 512], in_=bank(bk), func=AF.Copy),
                     reads=[Bbank[bk]], writes=[BkT[tc]])
            for tq in range(4):
                bk = next_pp()
                fns = []
                for j in range(4):
                    tt = tq * 4 + j
                    for kc in range(2):
                        fns.append(lambda e, kc=kc, tt=tt, j=j, bk=bk: e.matmul(bank(bk)[:, j * 128:(j + 1) * 128], v3(ckvn, 2)[:, kc, tt * 128:(tt + 1) * 128],
                                                                                wukv3[:, kc, h * 256 + 128:h * 256 + 256], start=(kc == 0), stop=(kc == 1)))
                S.op('pe', fns, reads=[Bwu, Bckvn[tq]], writes=[Bbank[bk]])
                S.op('act', lambda e, bk=bk, tq=tq: e.activation(out=Vt[:, tq * 512:(tq + 1) * 512], in_=bank(bk), func=AF.Copy),
                     reads=[Bbank[bk]], writes=[BVt[tq]])

        if stop_after == 'B2':
            return False
        scaleB = float(192 ** -0.5)
        prepB(0, 0)
        if stop_after == 'B3':
            return False
        if stop_after == 'B4':
            prepB_kv(0)
            return False
        for h in range(4):
            if h + 1 < 4:
                prepB(h + 1, (h + 1) % 2)
            prepB_kv(h)
            attn_dense(oT, BoTl, 8 + h, h % 2, h % 2, scaleB, kpe=kpe, qpe=qpe[h % 2], Bkpe=Bkpe, Bqpe=Bqpe[h % 2])
        if stop_after == 'B':
            return False

        Balias = Bcqn + Bckvn + [Bkpe, Bwu] + Bqpe[0] + Bqpe[1]

        def alias_guard(mc):
            for b in Balias:
                for c in range(4):
                    for k, v in b.r.items():
                        if BoTl[mc][c].r.get(k, 0) < v:
                            BoTl[mc][c].r[k] = v
                    if b.w is not None:
                        k, v = b.w
                        if BoTl[mc][c].r.get(k, 0) < v:
                            BoTl[mc][c].r[k] = v
        for mc in list(range(0, 8)) + list(range(12, 16)):
            alias_guard(mc)

        S.dma('sp', lambda e: e.dma_start(out=cmask, in_=cmask_d), 'cmask', writes=[Bcmask])
        scaleC = float(128 ** -0.5)

        def prepC_bias(h, bslot):
            S.dma('pool', lambda e: e.dma_start(out=biasm[bslot], in_=cbias_d[(li * 4 + h) * 128:(li * 4 + h + 1) * 128, :]), 'bias%d' % bslot, writes=[Bbiasm[bslot]])
            S.op('pool', lambda e: e.tensor_tensor(out=biasm[bslot], in0=biasm[bslot], in1=cmask, op=ALU.add), reads=[Bbiasm[bslot], Bcmask], writes=[Bbiasm[bslot]])

        def prepC_q(h, slot):
            g0 = gbase + 11 + 4 * h
            proj_block_fm(hT, BhTl, g0, 128,
                          lambda tc, bk: S.op('act', lambda e: e.activation(out=qT[slot][:, tc * 512:(tc + 1) * 512], in_=bank(bk), func=AF.Copy, scale=scaleC),
                                              reads=[Bbank[bk]], writes=[BqT[slot][tc]]))

        def prepC_kvg(h, slot):
            g0 = gbase + 11 + 4 * h
            proj_block_fm(hT, BhTl, g0 + 1, 128,
                          lambda tc, bk: S.op('dve', lambda e: e.tensor_copy(out=kT[:, tc * 512:(tc + 1) * 512], in_=bank(bk)),
                                              reads=[Bbank[bk]], writes=[BkT[tc]]))
            proj_block_tm(hT, BhTl, g0 + 2, Vt, BVt)
            proj_block_fm(hT, BhTl, g0 + 3, 128,
                          lambda tc, bk: silu_chunk(bk, gT[slot][:, tc * 512:(tc + 1) * 512], BgT[slot][tc], tc % 2))

        for h in range(4):
            prepC_bias(h, h % 2)
            prepC_q(h, h % 2)
            if stop_after == 'C1':
                break
            prepC_kvg(h, h % 2)
            if stop_after == 'C2':
                break
            attn_window(oT, BoTl, 12 + h, h % 2, h % 2, h % 2)
            if stop_after == 'C3':
                break
        if stop_after in ('C1', 'C2', 'C3'):
            return False
        if stop_after == 'C':
            return False

        scaleA = float(128 ** 0.5)
        aq = vecs[:, vb + 0:vb + 1]
        ak = vecs[:, vb + 1:vb + 2]

        def prepA_kv(g):
            g0 = gbase + 27 + 10 * g
            proj_block_fm(hT, BhTl, g0, 128,
                          lambda tc, bk: rope_chunk(bk, 128, 64, tc, TAc, TAs, permA, ak, 128.0, kT[:, tc * 512:(tc + 1) * 512], BkT[tc], tc % 2))
            proj_block_tm(hT, BhTl, g0 + 1, Vt, BVt)

        def prepA_q(h, slot):
            g = h // 4
            g0 = gbase + 27 + 10 * g + 2 + 2 * (h % 4)
            proj_block_fm(hT, BhTl, g0, 128,
                          lambda tc, bk: rope_chunk(bk, 128, 64, tc, TAc, TAs, permA, aq, 128.0, qT[slot][:, tc * 512:(tc + 1) * 512], BqT[slot][tc], tc % 2))
            proj_block_fm(hT, BhTl, g0 + 1, 128,
                          lambda tc, bk: silu_chunk(bk, gT[slot][:, tc * 512:(tc + 1) * 512], BgT[slot][tc], tc % 2))

        w3o = v3(hT, 16)
        wout_issued = {'v': False}

        def issue_wout():
            wout_issued['v'] = True
            for mc in range(16):
                S.dma('pool', lambda e, mc=mc: e.dma_start(out=w3o[:, mc, :], in_=wout_d[(li * 16 + mc) * 128:(li * 16 + mc + 1) * 128, :]),
                      'wout', writes=BhTl if mc == 0 else [])

        for g in range(2):
            prepA_kv(g)
            prepA_q(4 * g, 0)
            for hh in range(4):
                h = 4 * g + hh
                if hh + 1 < 4:
                    prepA_q(h + 1, (hh + 1) % 2)
                    if h + 1 == 7:
                        issue_wout()
                attn_dense(oT, BoTl, h, hh % 2, hh % 2, scaleA)
        if stop_after == 'A':
            return False

        flush_pend()
        if not wout_issued['v']:
            issue_wout()
        Bwout = Buf("wout")
        Bwout.w = ('wout', S.dcnt['wout'])
        S.barrier()
        last = (li == L - 1)
        load_gains(li + 1 if not last else li, True) if not last else \
            S.dma('sp', lambda e: e.dma_start(out=gpost_bc, in_=gpost_d[li:li + 1, :].broadcast_to([128, D])), 'gpost', writes=[Bgpost])
        src_d = x_d if li == 0 else out_d
        o3 = v3(oT, 16)
        ybank = {'i': 0}
        Bjunk = Buf("junk")
        pend1 = None
        pend2 = None
        Bot = [Buf("ot%d" % t) for t in range(16)]
        for tt in range(16):
            i2 = tt % 2
            tcq = tt // 4
            S.dma('sp', lambda e, tt=tt, i2=i2: e.dma_start(out=xt[i2], in_=src_d[tt * 128:(tt + 1) * 128, :]), 'xt%d' % i2,
                  reads=[Bout[tt]], writes=[Bxt[i2]])
            ssp = stat[:, 16 + 8 * i2:16 + 8 * i2 + 4]
            for nb in range(4):
                bk = ybank['i'] % 6
                ybank['i'] += 1
                fns = [lambda e, mc=mc, nb=nb, bk=bk, tt=tt: e.matmul(bank(bk), o3[:, mc, tt * 128:(tt + 1) * 128], w3o[:, mc, nb * 512:(nb + 1) * 512],
                                                                      start=(mc == 0), stop=(mc == 15)) for mc in range(16)]
                S.op('pe', fns, reads=[Bwout, Bot[tt]], writes=[Bbank[bk]])
                S.op('act', lambda e, nb=nb, bk=bk, i2=i2, ssp=ssp: e.activation(out=junk, in_=bank(bk), func=AF.Square, accum_out=ssp[:, nb:nb + 1]),
                     reads=[Bbank[bk]], writes=[Bjunk, Bst_y[i2]])
                S.op('dve', lambda e, nb=nb, bk=bk, i2=i2: e.tensor_copy(out=yraw[i2][:, nb * 512:(nb + 1) * 512], in_=bank(bk)),
                     reads=[Bbank[bk]], writes=[Byraw[i2]])
            ss = stat[:, 32 + 4 * i2:32 + 4 * i2 + 1]
            rr = stat[:, 32 + 4 * i2 + 1:32 + 4 * i2 + 2]
            S.op('dve', lambda e, ss=ss, ssp=ssp: e.tensor_reduce(out=ss, in_=ssp, axis=mybir.AxisListType.X, op=ALU.add), reads=[Bst_y[i2]], writes=[Bst_y[i2]])
            S.op('act', lambda e, ss=ss, rr=rr: e.activation(out=rr, in_=ss, func=AF.Ln, scale=1.0 / D, bias=EPS), reads=[Bst_y[i2]], writes=[Bst_y[i2]])
            S.op('act', lambda e, rr=rr: e.activation(out=rr, in_=rr, func=AF.Exp, scale=-0.5), reads=[Bst_y[i2]], writes=[Bst_y[i2]])
            S.op('dve', lambda e, i2=i2, rr=rr: e.scalar_tensor_tensor(out=yraw[i2], in0=yraw[i2], scalar=rr, in1=gpost_bc, op0=ALU.mult, op1=ALU.mult),
                 reads=[Byraw[i2], Bst_y[i2], Bgpost], writes=[Byraw[i2]])
            S.op('pool', lambda e, i2=i2: e.tensor_tensor(out=xt[i2], in0=xt[i2], in1=yraw[i2], op=ALU.add), reads=[Bxt[i2], Byraw[i2]], writes=[Bxt[i2]])
            S.dma('sp', lambda e, tt=tt, i2=i2: e.dma_start(out=out_d[tt * 128:(tt + 1) * 128, :], in_=xt[i2]), 'st%d' % i2,
                  reads=[Bxt[i2]], writes=[Bout[tt]])
            if not last:
                if pend2 is not None:
                    hT_part2(pend2, oT, [BhT[1 - hp][pend2], Bot[pend2]], pend2 % 2)
                    pend2 = None
                if pend1 is not None:
                    hT_part1(xt[pend1 % 2], Bxt[pend1 % 2], pend1 % 2, 4 * (pend1 % 2))
                    pend2 = pend1
                pend1 = tt
        if not last:
            if pend2 is not None:
                hT_part2(pend2, oT, [BhT[1 - hp][pend2], Bot[pend2]], pend2 % 2)
            hT_part1(xt[pend1 % 2], Bxt[pend1 % 2], pend1 % 2, 4 * (pend1 % 2))
            hT_part2(pend1, oT, [BhT[1 - hp][pend1], Bot[pend1]], pend1 % 2)
        return True


    for li in range(L):
        if not do_layer(li, hp):
            break
        hp = 1 - hp

    if 'oT' in dbg_d:
        S.barrier()
        src = bufs2[1]
        for kc in dbg.get('oT_chunks', range(16)):
            S.op('dve', lambda e, kc=kc: e.tensor_copy(out=yraw[0], in_=v3(src, 16)[:, kc, :]), reads=[], writes=[Byraw[0]])
            S.dma('sp', lambda e, kc=kc: e.dma_start(out=dbg_d['oT'][kc * 128:(kc + 1) * 128, :], in_=yraw[0]), 'dbg', reads=[Byraw[0]])
    S.barrier()
    S.emit(nc)
    st.close()
    return nc


def _prep_shared(norm_pre, norm_post, w_in, a_q_norm, a_k_norm, b_q_norm, b_kv_norm, b_w_uq, b_w_ukv, c_rpb, w_out):
    L = w_in.shape[0]
    cols = _block_cols()
    win = np.empty((L, NBLK, 128, 16, 128), np.float32)
    for l in range(L):
        g = w_in[l][:, cols.reshape(-1)].reshape(16, 128, NBLK, 128)
        win[l] = g.transpose(2, 1, 0, 3)
    win = win.reshape(L * NBLK * 128, 2048)
    wuq = np.ascontiguousarray(b_w_uq.reshape(L, 4, 128, 768).transpose(0, 2, 1, 3)).reshape(L * 128, 4 * 768)
    wukv = np.ascontiguousarray(b_w_ukv.reshape(L, 2, 128, 1024).transpose(0, 2, 1, 3)).reshape(L * 128, 2 * 1024)
    wout = np.ascontiguousarray(w_out.reshape(L * 16 * 128, 2048))
    vecs = np.zeros((128, L * 8), np.float32)
    for l in range(L):
        vecs[:, l * 8 + 0] = a_q_norm[l]
        vecs[:, l * 8 + 1] = a_k_norm[l]
        vecs[:, l * 8 + 2:l * 8 + 6] = b_q_norm[l].reshape(4, 128).T
        vecs[:, l * 8 + 6:l * 8 + 8] = b_kv_norm[l].reshape(2, 128).T
    dr, dc, va = _c_bias_index()
    cb = c_rpb[:, :, dr, dc]
    cb = np.where(va[None, None], cb, np.float32(0.0))
    cbias = np.ascontiguousarray(cb.transpose(0, 1, 3, 2, 4)).reshape(L * 4 * 128, 21 * 128).astype(np.float32)
    cmats, tabs, cmask = _const_tables()
    return {"win": win, "wuq": wuq, "wukv": wukv, "wout": wout,
            "gpre": np.ascontiguousarray(norm_pre, dtype=np.float32), "gpost": np.ascontiguousarray(norm_post, dtype=np.float32),
            "vecs": vecs, "cbias": cbias, "cmask": cmask, "cmats": cmats, "ctabs": tabs}


def kernel(x, norm_pre, norm_post, w_in, a_q_norm, a_k_norm, b_q_norm, b_kv_norm, b_w_uq, b_w_ukv, c_rpb, w_out):
    args = [np.asarray(a, dtype=np.float32) for a in (norm_pre, norm_post, w_in, a_q_norm, a_k_norm, b_q_norm, b_kv_norm, b_w_uq, b_w_ukv, c_rpb, w_out)]
    x = np.asarray(x, dtype=np.float32)
    shared = _prep_shared(*args)
    nc = bass.Bass("TRN2", target_bir_lowering=False)
    build(nc, L=DEPTH)
    n = x.shape[0]
    in_maps = []
    for b in range(n):
        m = dict(shared)
        m["x"] = np.ascontiguousarray(x[b])
        in_maps.append(m)
    res = run_bass_kernel_spmd(nc, in_maps, core_ids=list(range(n)))
    return np.stack([np.asarray(r["out"], dtype=np.float32) for r in res.results], axis=0)
```

```python
import contextlib
import numpy as np
import ml_dtypes
import concourse.bass as bass
import concourse.mybir as mybir
from concourse.bass_utils import run_bass_kernel_spmd

F32 = mybir.dt.float32
BF16 = mybir.dt.bfloat16
U8 = mybir.dt.uint8
AF = mybir.ActivationFunctionType
ALU = mybir.AluOpType
ENGS = ('pe', 'act', 'dve', 'pool', 'sp')

S_TOK = 2048
D = 2048
DEPTH = 2
NBLK = 47
EPS = 1e-6
NEG = -80.0


class Buf:
    __slots__ = ('name', 'w', 'r', 'excl')

    def __init__(s, name, excl=False):
        s.name = name
        s.w = None
        s.r = {}
        s.excl = excl


class Sched:
    def __init__(s):
        s.q = {e: [] for e in ENGS}
        s.cnt = {e: 0 for e in ENGS}
        s.seen = {e: {} for e in ENGS}
        s.dcnt = {}

    def _waits(s, eng, reads, writes):
        need = {}

        def add(tok):
            if tok is None:
                return
            k, v = tok
            if need.get(k, 0) < v:
                need[k] = v
        for b in reads:
            add(b.w)
            if b.excl:
                for k, v in b.r.items():
                    if k != eng:
                        add((k, v))
        for b in writes:
            if b.w is not None and not (eng == 'pe' and b.w[0] == 'pe'):
                add(b.w)
            for k, v in b.r.items():
                if k != eng:
                    add((k, v))
        out = []
        for k, v in need.items():
            if s.seen[eng].get(k, 0) < v:
                s.seen[eng][k] = v
                out.append((k, v))
        return out

    def _mark(s, tok, reads, writes):
        for b in reads:
            if b.r.get(tok[0], 0) < tok[1]:
                b.r[tok[0]] = tok[1]
        for b in writes:
            b.w = tok
            b.r = {}

    def op(s, eng, fns, reads=(), writes=(), fuse=False):
        if not isinstance(fns, (list, tuple)):
            fns = [fns]
        waits = s._waits(eng, reads, writes)
        s.cnt[eng] += 1
        tok = (eng, s.cnt[eng])
        s.q[eng].append((waits, list(fns), (eng, 1), fuse))
        s._mark(tok, reads, writes)
        return tok

    def dma(s, eng, fn, key, reads=(), writes=()):
        waits = s._waits(eng, reads, writes)
        s.dcnt[key] = s.dcnt.get(key, 0) + 16
        tok = (key, s.dcnt[key])
        s.q[eng].append((waits, [fn], (key, 16), False))
        s._mark(tok, reads, writes)
        return tok

    def barrier(s):
        toks = [(e, s.cnt[e]) for e in ENGS if s.cnt[e] > 0] + list(s.dcnt.items())
        for e in ENGS:
            waits = []
            for k, v in toks:
                if k == e:
                    continue
                if s.seen[e].get(k, 0) < v:
                    s.seen[e][k] = v
                    waits.append((k, v))
            if waits:
                s.q[e].append((waits, [], None, False))

    def emit(s, nc):
        keys = list(ENGS) + list(s.dcnt.keys())
        with contextlib.ExitStack() as st:
            sems = {k: st.enter_context(nc.semaphore("s_" + str(k))) for k in keys}
            block = st.enter_context(nc.Block())

            def run(e, name):
                for waits, fns, inc, fuse in s.q[name]:
                    att = None
                    if fuse and waits and fns:
                        att = waits[-1]
                        waits = waits[:-1]
                    for k, v in waits:
                        e.wait_ge(sems[k], v)
                    ins = None
                    for i, fn in enumerate(fns):
                        ins = fn(e)
                        if i == 0 and att is not None:
                            ins._wait_ge(sems[att[0]], att[1])
                    if inc is not None and ins is not None:
                        ins.then_inc(sems[inc[0]], inc[1])

            @block.tensor
            def _(e):
                run(e, 'pe')

            @block.scalar
            def _(e):
                run(e, 'act')

            @block.vector
            def _(e):
                run(e, 'dve')

            @block.gpsimd
            def _(e):
                run(e, 'pool')

            @block.sync
            def _(e):
                run(e, 'sp')


def _block_cols():
    blks = []
    for i in range(4):
        blks.append(list(range(2560 + 128 * i, 2560 + 128 * (i + 1))))
    for i in range(2):
        blks.append(list(range(3072 + 128 * i, 3072 + 128 * (i + 1))))
    blks.append(list(range(3328, 3392)) + list(range(3328, 3392)))
    for h in range(4):
        blks.append(list(range(3392 + 128 * h, 3392 + 128 * (h + 1))))
    for h in range(4):
        for base in (3904, 4416, 4928, 5440):
            blks.append(list(range(base + 128 * h, base + 128 * (h + 1))))
    for g in range(2):
        blks.append(list(range(1024 + 128 * g, 1024 + 128 * (g + 1))))
        blks.append(list(range(1280 + 128 * g, 1280 + 128 * (g + 1))))
        for h in range(4 * g, 4 * g + 4):
            blks.append(list(range(128 * h, 128 * (h + 1))))
            blks.append(list(range(1536 + 128 * h, 1536 + 128 * (h + 1))))
    assert len(blks) == NBLK
    return np.array(blks, dtype=np.int64)


def _c_slots():
    slots = []
    for u in range(16):
        if 2 <= u <= 13:
            slots.append([(u + d, d + 2) for d in range(-2, 3)])
        else:
            t0 = 0 if u < 2 else 12
            sp = {0: 0, 1: 1, 14: 2, 15: 3}[u]
            slots.append([(t0 + j, 5 + 4 * sp + j) for j in range(4)])
    return slots


def _c_bias_index():
    ent = []
    for d in range(-2, 3):
        ent.append((6, 6 + d))
    for u in (0, 1, 14, 15):
        t0 = 0 if u < 2 else 12
        for j in range(4):
            ent.append((u, t0 + j))
    dr = np.zeros((21, 128, 128), np.int64)
    dc = np.zeros((21, 128, 128), np.int64)
    va = np.zeros((21, 128, 128), bool)
    kl = np.arange(128)
    for e, (u, t) in enumerate(ent):
        q = u * 128 + kl
        k = t * 128 + kl
        qr, qc = q // 64, q % 64
        kr, kc = k // 64, k % 64
        r0 = np.clip(qr - 4, 0, 24)
        c0 = np.clip(qc - 8, 0, 48)
        ddr = kr[:, None] - qr[None, :]
        ddc = kc[:, None] - qc[None, :]
        ok = ((kr[:, None] >= r0[None, :]) & (kr[:, None] < r0[None, :] + 8) &
              (kc[:, None] >= c0[None, :]) & (kc[:, None] < c0[None, :] + 16))
        va[e] = ok
        dr[e] = np.where(ok, ddr + 7, 0)
        dc[e] = np.where(ok, ddc + 15, 0)
    return dr, dc, va


def _const_tables():
    ident = np.eye(128, dtype=np.float32)
    ones = np.ones((128, 128), np.float32)
    permA = np.zeros((128, 128), np.float32)
    for m in range(128):
        p = m + 32 if (m % 64) < 32 else m - 32
        permA[p, m] = 1.0
    permB = np.zeros((128, 128), np.float32)
    for m in range(64):
        p = m + 16 if (m % 32) < 16 else m - 16
        permB[p, m] = 1.0
    cmats = np.concatenate([ident, ones, permA, permB], axis=1).astype(ml_dtypes.bfloat16)
    theta = np.float32(10000.0)
    tabs = np.zeros((128, 4 * 64), np.float32)
    invA = (1.0 / (theta ** (np.arange(0, 64, 2, dtype=np.float32) / np.float32(64)))).astype(np.float32)
    invB = (1.0 / (theta ** (np.arange(0, 32, 2, dtype=np.float32) / np.float32(32)))).astype(np.float32)
    pos = np.arange(64, dtype=np.float32)
    for d in range(128):
        i = d % 32
        ang = (pos * invA[i]).astype(np.float32)
        sign = -1.0 if (d % 64) < 32 else 1.0
        tabs[d, 0:64] = np.cos(ang)
        tabs[d, 64:128] = sign * np.sin(ang)
    for d in range(64):
        i = d % 16
        ang = (pos * invB[i]).astype(np.float32)
        sign = -1.0 if (d % 32) < 16 else 1.0
        tabs[d, 128:192] = np.cos(ang)
        tabs[d, 192:256] = sign * np.sin(ang)
    _, _, va = _c_bias_index()
    cmask = np.where(va, 0.0, NEG).astype(np.float32)
    cmask = np.ascontiguousarray(cmask.transpose(1, 0, 2)).reshape(128, 21 * 128).astype(ml_dtypes.bfloat16)
    return cmats, tabs, cmask


def build(nc, L=DEPTH, dbg=None):
    dbg = dbg or {}
    stop_after = dbg.get('stop_after')

    def dram(name, shape, dt, kind):
        return nc.dram_tensor(name, shape, dt, kind=kind).ap()
    x_d = dram("x", [S_TOK, D], F32, "ExternalInput")
    win_d = dram("win", [L * NBLK * 128, 2048], F32, "ExternalInput")
    wuq_d = dram("wuq", [L * 128, 4 * 768], F32, "ExternalInput")
    wukv_d = dram("wukv", [L * 128, 2 * 1024], F32, "ExternalInput")
    wout_d = dram("wout", [L * 16 * 128, 2048], F32, "ExternalInput")
    gpre_d = dram("gpre", [L, D], F32, "ExternalInput")
    gpost_d = dram("gpost", [L, D], F32, "ExternalInput")
    vecs_d = dram("vecs", [128, L * 8], F32, "ExternalInput")
    cbias_d = dram("cbias", [L * 4 * 128, 21 * 128], F32, "ExternalInput")
    cmask_d = dram("cmask", [128, 21 * 128], BF16, "ExternalInput")
    cmats_d = dram("cmats", [128, 4 * 128], BF16, "ExternalInput")
    ctabs_d = dram("ctabs", [128, 4 * 64], F32, "ExternalInput")
    out_d = dram("out", [S_TOK, D], F32, "ExternalOutput")
    dbg_d = {}
    for name, shape in dbg.get('outs', {}).items():
        dbg_d[name] = dram(name, shape, F32, "ExternalOutput")

    S = Sched()
    st = contextlib.ExitStack()
    ARENA = 211968
    arena = st.enter_context(nc.sbuf_tensor("arena", [128, ARENA], U8))
    psum = st.enter_context(nc.psum_tensor("psum", [128, 4096], F32))
    psum_bf = psum[:, :].bitcast(BF16)

    def carve(off, nbytes, dt):
        assert off + nbytes <= ARENA, (off, nbytes)
        return arena[:, off:off + nbytes].bitcast(dt)

    bufs2 = [carve(0, 65536, BF16), carve(65536, 65536, BF16)]
    off = 131072
    cm = carve(off, 1024, BF16); off += 1024
    ident, ones, permA, permB = (cm[:, i * 128:(i + 1) * 128] for i in range(4))
    tabs = carve(off, 1024, F32); off += 1024
    vecs = carve(off, 64, F32); off += 64
    stat = carve(off, 256, F32); off += 256
    off = 134144
    NW = 3
    wsl = [carve(off + i * 4096, 4096, BF16) for i in range(NW)]; off += NW * 4096
    OV = off
    kT = carve(off, 4096, BF16); off += 4096
    Vt = carve(off, 4096, BF16); off += 4096
    qT = [carve(off + i * 4096, 4096, BF16) for i in range(2)]; off += 8192
    gT = [carve(off + i * 4096, 4096, BF16) for i in range(2)]; off += 8192
    Psb = [carve(off + i * 2048, 2048, BF16) for i in range(2)]; off += 4096
    sqs = [carve(off + i * 1024, 1024, BF16) for i in range(2)]; off += 2048
    qgs = [carve(off + i * 1024, 1024, BF16) for i in range(2)]; off += 2048
    rsb = [carve(off + i * 2048, 2048, F32) for i in range(2)]; off += 4096
    t1s = [carve(off + i * 2048, 2048, F32) for i in range(2)]; off += 4096
    t2s = [carve(off + i * 2048, 2048, F32) for i in range(2)]; off += 4096
    rden = carve(off, 2048, F32); off += 2048
    otmp = carve(off, 2048, F32); off += 2048
    cmask = carve(off, 5376, BF16); off += 5376
    biasm = [carve(off + i * 5376, 5376, BF16) for i in range(2)]; off += 10752
    assert off <= ARENA, off
    off = OV
    gpre_bc = carve(off, 8192, F32); off += 8192
    gpost_bc = carve(off, 8192, F32); off += 8192
    xt = [carve(off + i * 8192, 8192, F32) for i in range(2)]; off += 16384
    yraw = [carve(off + i * 8192, 8192, F32) for i in range(2)]; off += 16384
    xn = [carve(off + i * 4096, 4096, BF16) for i in range(2)]; off += 8192
    junk = carve(off, 1024, BF16); off += 1024
    assert off <= ARENA

    def bank(i, n=1):
        return psum[:, i * 512:(i + n) * 512]
    Bbank = [Buf("bank%d" % i, excl=True) for i in range(8)]

    Bconst = Buf("const")
    BhT = [[Buf("hT%d_%d" % (p, t)) for t in range(16)] for p in range(2)]
    BoT = [[[Buf("oT%d_%d_%d" % (p, m, c)) for c in range(4)] for m in range(16)] for p in range(2)]
    Bw = [Buf("w%d" % i) for i in range(NW)]
    BkT = [Buf("kT%d" % c) for c in range(4)]
    BVt = [Buf("Vt%d" % c) for c in range(4)]
    BqT = [[Buf("qT%d_%d" % (i, c)) for c in range(4)] for i in range(2)]
    BgT = [[Buf("gT%d_%d" % (i, c)) for c in range(4)] for i in range(2)]
    BP = [Buf("P%d" % i) for i in range(2)]
    Bsq = [Buf("sq%d" % i) for i in range(2)]
    Bqg = [Buf("qg%d" % i) for i in range(2)]
    Brs = [Buf("rs%d" % i) for i in range(2)]
    Bt1 = [Buf("t1%d" % i) for i in range(2)]
    Bt2 = [Buf("t2%d" % i) for i in range(2)]
    Brden = Buf("rden"); Botmp = Buf("otmp")
    Bcmask = Buf("cmask"); Bbiasm = [Buf("biasm%d" % i) for i in range(2)]
    Bgpre = Buf("gpre"); Bgpost = Buf("gpost")
    Bxt = [Buf("xt%d" % i) for i in range(2)]
    Byraw = [Buf("yraw%d" % i) for i in range(2)]
    Bxn = [Buf("xn%d" % i) for i in range(2)]
    Bst_h = [Buf("sth%d" % i) for i in range(2)]
    Bst_y = [Buf("sty%d" % i) for i in range(2)]
    Bout = [Buf("outrow%d" % t) for t in range(16)]

    S.dma('sp', lambda e: e.dma_start(out=cm, in_=cmats_d), 'c0', writes=[Bconst])
    S.dma('sp', lambda e: e.dma_start(out=tabs, in_=ctabs_d), 'c1', writes=[Bconst])
    S.dma('sp', lambda e: e.dma_start(out=vecs[:, 0:L * 8], in_=vecs_d), 'c2', writes=[Bconst])
    TAc, TAs, TBc, TBs = (tabs[:, i * 64:(i + 1) * 64] for i in range(4))
    S.op('pool', lambda e: e.memset(stat[:, 63:64], -1.0), reads=[], writes=[Bconst])

    def v3(ap, k):
        return ap.rearrange("p (k n) -> p k n", k=k)

    wstate = {'issued': 0}

    def w_issue_upto(gi):
        while wstate['issued'] <= gi and wstate['issued'] < L * NBLK:
            i = wstate['issued']
            sl = i % NW
            S.dma('pool', lambda e, i=i, sl=sl: e.dma_start(out=wsl[sl], in_=win_d[i * 128:(i + 1) * 128, :]),
                  'w%d' % sl, writes=[Bw[sl]])
            wstate['issued'] += 1

    def wget(gi):
        w_issue_upto(gi + NW - 1)
        sl = gi % NW
        return v3(wsl[sl], 16), Bw[sl]

    pp_state = {'i': 0}

    def next_pp():
        i = pp_state['i']
        pp_state['i'] ^= 1
        return 6 + i

    def proj_fm(hT, BhTl, w3, Bwb, M, tc, bk):
        h3 = v3(hT, 16)
        fns = []
        for kc in range(16):
            fns.append(lambda e, kc=kc: e.matmul(bank(bk)[0:M, :], w3[:, kc, 0:M], h3[:, kc, tc * 512:(tc + 1) * 512],
                                                 start=(kc == 0), stop=(kc == 15)))
        S.op('pe', fns, reads=[Bwb] + BhTl[tc * 4:(tc + 1) * 4], writes=[Bbank[bk]])

    def rstd_from_bank(bk, P, r_ap, Br, n, reads_extra=()):
        S.op('act', lambda e: e.activation(out=r_ap, in_=bank(bk)[0:P, :], func=AF.Ln, scale=1.0 / n, bias=EPS),
             reads=[Bbank[bk]] + list(reads_extra), writes=[Br])
        S.op('act', lambda e: e.activation(out=r_ap, in_=r_ap, func=AF.Exp, scale=-0.5), reads=[Br], writes=[Br])

    def rope_chunk(src_bk, P, half, tc, cosT, sinT, perm, g_ap, norm_n, dst_ap, Bdst, i2, scale_mult=None):
        sq, qg, r, t1, t2 = sqs[i2][0:P, :], qgs[i2][0:P, :], rsb[i2][0:P, :], t1s[i2][0:P, :], t2s[i2][0:P, :]
        ss_bk, sw_bk = (0, 2) if i2 == 0 else (1, 3)
        src = bank(src_bk)[0:P, :]
        if norm_n:
            S.op('act', lambda e: e.activation(out=sq, in_=src, func=AF.Square), reads=[Bbank[src_bk]], writes=[Bsq[i2]])
        if g_ap is not None:
            S.op('dve', lambda e: e.tensor_scalar(out=qg, in0=src, scalar1=g_ap, scalar2=None, op0=ALU.mult),
                 reads=[Bbank[src_bk], Bconst], writes=[Bqg[i2]])
        else:
            S.op('dve', lambda e: e.tensor_copy(out=qg, in_=src), reads=[Bbank[src_bk]], writes=[Bqg[i2]])
        if norm_n:
            S.op('pe', lambda e: e.matmul(bank(ss_bk)[0:P, :], ones[0:P, 0:P], sq, start=True, stop=True),
                 reads=[Bsq[i2], Bconst], writes=[Bbank[ss_bk]])
        S.op('pe', lambda e: e.matmul(bank(sw_bk)[0:P, :], perm[0:P, 0:P], qg, start=True, stop=True),
             reads=[Bqg[i2], Bconst], writes=[Bbank[sw_bk]])
        if norm_n:
            S.op('act', lambda e: e.activation(out=r, in_=bank(ss_bk)[0:P, :], func=AF.Ln, scale=1.0, bias=norm_n * EPS),
                 reads=[Bbank[ss_bk]], writes=[Brs[i2]])
            S.op('act', lambda e: e.activation(out=r, in_=r, func=AF.Exp, scale=-0.5), reads=[Brs[i2]], writes=[Brs[i2]])
        sw = bank(sw_bk)
        lo, hi = slice(0, half), slice(half, P)

        def rowb(tab):
            return tab[lo, tc * 8:(tc + 1) * 8].unsqueeze(2).broadcast_to([half, 8, 64])

        def colb(tab):
            return tab[hi, 0:64].unsqueeze(1).broadcast_to([P - half, 8, 64])

        def v8(ap):
            return ap.rearrange("p (a b) -> p a b", a=8)
        S.op('pool', [lambda e: e.tensor_tensor(out=v8(t1s[i2][lo, :]), in0=v8(qgs[i2][lo, :]), in1=rowb(cosT), op=ALU.mult),
                      lambda e: e.tensor_tensor(out=v8(t1s[i2][hi, :]), in0=v8(qgs[i2][hi, :]), in1=colb(cosT), op=ALU.mult)],
             reads=[Bqg[i2], Bconst], writes=[Bt1[i2]])
        S.op('dve', [lambda e: e.tensor_tensor(out=v8(t2s[i2][lo, :]), in0=v8(sw[lo, :]), in1=rowb(sinT), op=ALU.mult),
                     lambda e: e.tensor_tensor(out=v8(t2s[i2][hi, :]), in0=v8(sw[hi, :]), in1=colb(sinT), op=ALU.mult)],
             reads=[Bbank[sw_bk], Bconst], writes=[Bt2[i2]])
        if norm_n:
            S.op('dve', lambda e: e.tensor_tensor(out=t1, in0=t1, in1=t2, op=ALU.add), reads=[Bt1[i2], Bt2[i2]], writes=[Bt1[i2]])
            S.op('dve', lambda e: e.tensor_tensor(out=dst_ap, in0=t1, in1=r, op=ALU.mult),
                 reads=[Bt1[i2], Brs[i2]], writes=[Bdst])
        else:
            S.op('dve', lambda e: e.tensor_tensor(out=dst_ap, in0=t1, in1=t2, op=ALU.add), reads=[Bt1[i2], Bt2[i2]], writes=[Bdst])

    def silu_chunk(src_bk, dst_ap, Bdst, i2):
        S.op('act', lambda e: e.activation(out=dst_ap, in_=bank(src_bk), func=AF.Silu), reads=[Bbank[src_bk]], writes=[Bdst])

    pend = {'f': None}

    def flush_pend():
        f = pend['f']
        pend['f'] = None
        if f is not None:
            f()

    def proj_block_fm(hT, BhTl, gi, M, consume):
        w3, Bwb = wget(gi)
        for tc in range(4):
            bk = next_pp()
            proj_fm(hT, BhTl, w3, Bwb, M, tc, bk)
            flush_pend()
            pend['f'] = (lambda tc=tc, bk=bk: consume(tc, bk))

    def proj_block_tm(hT, BhTl, gi, dst, Bdstl):
        w3, Bwb = wget(gi)
        h3 = v3(hT, 16)
        d3 = v3(dst, 16)
        for tq in range(4):
            bk = next_pp()
            fns = []
            for j in range(4):
                tt = tq * 4 + j
                for kc in range(16):
                    fns.append(lambda e, kc=kc, tt=tt, j=j, bk=bk: e.matmul(bank(bk)[:, j * 128:(j + 1) * 128], h3[:, kc, tt * 128:(tt + 1) * 128],
                                                                      w3[:, kc, :], start=(kc == 0), stop=(kc == 15)))
            S.op('pe', fns, reads=[Bwb] + BhTl[tq * 4:(tq + 1) * 4], writes=[Bbank[bk]])
            flush_pend()
            S.op('act', lambda e, tq=tq, bk=bk: e.activation(out=dst[:, tq * 512:(tq + 1) * 512], in_=bank(bk), func=AF.Copy),
                 reads=[Bbank[bk]], writes=[Bdstl[tq]])

    ep_pend = {'f': None}

    def flush_ep():
        f = ep_pend['f']
        ep_pend['f'] = None
        if f is not None:
            f()

    def attn_epilogue(oTbuf, BoTl, mc, qc, gslot, ob, db):
        o3 = v3(oTbuf, 16)
        S.op('act', lambda e: e.activation(out=rden, in_=bank(db), func=AF.Ln), reads=[Bbank[db]], writes=[Brden])
        S.op('act', lambda e: e.activation(out=rden, in_=rden, func=AF.Exp, scale=-1.0), reads=[Brden], writes=[Brden])
        S.op('dve', lambda e: e.tensor_tensor(out=otmp, in0=bank(ob), in1=rden, op=ALU.mult), reads=[Bbank[ob], Brden], writes=[Botmp])
        S.op('pool', lambda e: e.tensor_tensor(out=o3[:, mc, qc * 512:(qc + 1) * 512], in0=otmp, in1=gT[gslot][:, qc * 512:(qc + 1) * 512], op=ALU.mult),
             reads=[Botmp, BgT[gslot][qc]], writes=[BoTl[mc][qc]])

    def attn_dense(oTbuf, BoTl, mc, qslot, gslot, scale, kpe=None, qpe=None, Bkpe=None, Bqpe=None):
        q_ap = qT[qslot]
        seq = [(qc, kp) for qc in range(4) for kp in range(8)]

        def s_mm(idx):
            qc, kp = seq[idx]
            sb = idx % 2
            fns = []
            for j in range(2):
                kt = 2 * kp + j
                o_ = bank(2 * sb + j)
                if kpe is None:
                    fns.append(lambda e, o_=o_, kt=kt, qc=qc: e.matmul(o_, kT[:, kt * 128:(kt + 1) * 128], q_ap[:, qc * 512:(qc + 1) * 512], start=True, stop=True))
                else:
                    fns.append(lambda e, o_=o_, kt=kt, qc=qc: e.matmul(o_, kT[:, kt * 128:(kt + 1) * 128], q_ap[:, qc * 512:(qc + 1) * 512], start=True, stop=False))
                    fns.append(lambda e, o_=o_, kt=kt, qc=qc: e.matmul(o_, kpe[0:64, kt * 128:(kt + 1) * 128], qpe[0:64, qc * 512:(qc + 1) * 512], start=False, stop=True))
            rd = [BkT[kp // 2], BqT[qslot][qc]]
            if kpe is not None:
                rd += [Bkpe, Bqpe[qc]]
            S.op('pe', fns, reads=rd, writes=[Bbank[2 * sb], Bbank[2 * sb + 1]], fuse=(idx >= 2))

        def rest(idx):
            qc, kp = seq[idx]
            sb = idx % 2
            ob, db = (4, 5) if qc % 2 == 0 else (6, 7)
            S.op('act', lambda e: e.activation(out=Psb[sb], in_=bank(2 * sb, 2), func=AF.Exp, scale=scale),
                 reads=[Bbank[2 * sb], Bbank[2 * sb + 1]], writes=[BP[sb]], fuse=True)
            if kp == 2:
                flush_ep()
            S.op('dve', lambda e: e.tensor_tensor(out=sqs[sb], in0=Psb[sb][:, 0:512], in1=Psb[sb][:, 512:1024], op=ALU.add),
                 reads=[BP[sb]], writes=[Bsq[sb]], fuse=True)
            fns = []
            for j in range(2):
                kt = 2 * kp + j
                first = (kp == 0 and j == 0)
                last = (kp == 7 and j == 1)
                fns.append(lambda e, kt=kt, j=j, first=first, last=last: e.matmul(bank(ob), Vt[:, kt * 128:(kt + 1) * 128], Psb[sb][:, j * 512:(j + 1) * 512], start=first, stop=last))
            rd = [BP[sb], BVt[kp // 2], Bconst]
            wr = [Bbank[ob]]
            if kp > 0:
                fns.append(lambda e, kp=kp: e.matmul(bank(db), ones, sqs[1 - sb], start=(kp == 1), stop=False))
                rd.append(Bsq[1 - sb])
                wr.append(Bbank[db])
            S.op('pe', fns, reads=rd, writes=wr, fuse=(idx >= 1))
            if kp == 7:
                S.op('pe', lambda e: e.matmul(bank(db), ones, sqs[sb], start=False, stop=True), reads=[Bsq[sb], Bconst], writes=[Bbank[db]])
                ep_pend['f'] = (lambda qc=qc, ob=ob, db=db: attn_epilogue(oTbuf, BoTl, mc, qc, gslot, ob, db))
        s_mm(0)
        flush_pend()
        for idx in range(len(seq)):
            if idx + 1 < len(seq):
                s_mm(idx + 1)
            rest(idx)
        flush_ep()

    slots_c = _c_slots()

    def attn_window(oTbuf, BoTl, mc, qslot, gslot, bslot):
        q_ap = qT[qslot]
        bm = biasm[bslot]

        def s_mm(u):
            sb = u % 2
            fns = []
            for s_, (t, eidx) in enumerate(slots_c[u]):
                o_ = psum[:, sb * 1024 + s_ * 128: sb * 1024 + (s_ + 1) * 128]
                fns.append(lambda e, o_=o_, t=t, u=u: e.matmul(o_, kT[:, t * 128:(t + 1) * 128], q_ap[:, u * 128:(u + 1) * 128], start=True, stop=False))
                fns.append(lambda e, o_=o_, eidx=eidx: e.matmul(o_, ident, bm[:, eidx * 128:(eidx + 1) * 128], start=False, stop=True))
            ts = sorted(set(t // 4 for t, _ in slots_c[u]))
            S.op('pe', fns, reads=[BkT[c] for c in ts] + [BqT[qslot][u // 4], Bbiasm[bslot], Bconst], writes=[Bbank[2 * sb], Bbank[2 * sb + 1]])

        def rest(u):
            sb = u % 2
            ns = len(slots_c[u])
            S.op('act', lambda e: e.activation(out=Psb[sb][:, 0:ns * 128], in_=psum[:, sb * 1024: sb * 1024 + ns * 128], func=AF.Exp, scale=1.0),
                 reads=[Bbank[2 * sb], Bbank[2 * sb + 1]], writes=[BP[sb]])
            uu = u % 4
            ob, db = (4, 5) if (u // 4) % 2 == 0 else (6, 7)
            fns = []
            for s_, (t, eidx) in enumerate(slots_c[u]):
                first = (s_ == 0)
                last = (s_ == ns - 1)
                fns.append(lambda e, t=t, s_=s_, first=first, last=last: e.matmul(bank(ob)[:, uu * 128:(uu + 1) * 128], Vt[:, t * 128:(t + 1) * 128], Psb[sb][:, s_ * 128:(s_ + 1) * 128], start=first, stop=last))
                fns.append(lambda e, s_=s_, first=first, last=last: e.matmul(bank(db)[:, uu * 128:(uu + 1) * 128], ones, Psb[sb][:, s_ * 128:(s_ + 1) * 128], start=first, stop=last))
            ts = sorted(set(t // 4 for t, _ in slots_c[u]))
            S.op('pe', fns, reads=[BP[sb], Bconst] + [BVt[c] for c in ts], writes=[Bbank[ob], Bbank[db]])
            if uu == 3:
                attn_epilogue(oTbuf, BoTl, mc, u // 4, gslot, ob, db)
        s_mm(0)
        flush_pend()
        for u in range(16):
            if u + 1 < 16:
                s_mm(u + 1)
            rest(u)

    def hT_part1(src_ap, Bsrc, i2, scol):
        ss = stat[:, scol:scol + 1]
        rr = stat[:, scol + 1:scol + 2]
        S.op('act', lambda e: e.activation(out=xn[i2], in_=src_ap, func=AF.Square, accum_out=ss), reads=[Bsrc], writes=[Bxn[i2], Bst_h[i2]])
        S.op('act', lambda e: e.activation(out=rr, in_=ss, func=AF.Ln, scale=1.0 / D, bias=EPS), reads=[Bst_h[i2]], writes=[Bst_h[i2]])
        S.op('act', lambda e: e.activation(out=rr, in_=rr, func=AF.Exp, scale=-0.5), reads=[Bst_h[i2]], writes=[Bst_h[i2]])
        S.op('dve', lambda e: e.scalar_tensor_tensor(out=xn[i2], in0=src_ap, scalar=rr, in1=gpre_bc, op0=ALU.mult, op1=ALU.mult),
             reads=[Bsrc, Bst_h[i2], Bgpre], writes=[Bxn[i2]])

    def hT_part2(tt, hTdst, Bdst_list, i2):
        pst = psum_bf[:, 6 * 1024: 8 * 1024]
        fns = [lambda e, kc=kc: e.transpose(pst[:, kc * 128:(kc + 1) * 128], xn[i2][:, kc * 128:(kc + 1) * 128], ident) for kc in range(16)]
        S.op('pe', fns, reads=[Bxn[i2], Bconst], writes=[Bbank[6], Bbank[7]])
        h3 = v3(hTdst, 16)
        S.op('dve', lambda e: e.tensor_copy(out=h3[:, :, tt * 128:(tt + 1) * 128], in_=pst.rearrange("p (k n) -> p k n", k=16)),
             reads=[Bbank[6], Bbank[7]], writes=Bdst_list)

    def make_hT_tile(src_ap, Bsrc, tt, hTdst, Bdst_list, li, i2, scol):
        hT_part1(src_ap, Bsrc, i2, scol)
        hT_part2(tt, hTdst, Bdst_list, i2)

    def load_gains(li, with_post):
        S.dma('sp', lambda e: e.dma_start(out=gpre_bc, in_=gpre_d[li:li + 1, :].broadcast_to([128, D])), 'gpre', writes=[Bgpre])
        if with_post:
            S.dma('sp', lambda e: e.dma_start(out=gpost_bc, in_=gpost_d[li - 1:li, :].broadcast_to([128, D])), 'gpost', writes=[Bgpost])

    hp = 0
    load_gains(0, False)
    w_issue_upto(NW - 1)
    xb4 = [xt[0], xt[1], yraw[0], yraw[1]]
    Bxb4 = [Bxt[0], Bxt[1], Byraw[0], Byraw[1]]
    xkeys = ['xt0', 'xt1', 'xp2', 'xp3']
    for tt in range(16):
        i2 = tt % 2
        i4 = tt % 4
        S.dma('sp', lambda e, tt=tt, i4=i4: e.dma_start(out=xb4[i4], in_=x_d[tt * 128:(tt + 1) * 128, :]), xkeys[i4], writes=[Bxb4[i4]])
        hT_part1(xb4[i4], Bxb4[i4], i2, 4 * i2)
        if tt > 0:
            hT_part2(tt - 1, bufs2[hp], [BhT[hp][tt - 1]], (tt - 1) % 2)
    hT_part2(15, bufs2[hp], [BhT[hp][15]], 1)

    if 'hT' in dbg_d:
        S.barrier()
        for kc in range(16):
            S.op('dve', lambda e, kc=kc: e.tensor_copy(out=yraw[0], in_=v3(bufs2[hp], 16)[:, kc, :]), reads=BhT[hp], writes=[Byraw[0]])
            S.dma('sp', lambda e, kc=kc: e.dma_start(out=dbg_d['hT'][kc * 128:(kc + 1) * 128, :], in_=yraw[0]), 'dbg', reads=[Byraw[0]])

    def do_layer(li, hp):
        if stop_after == 'pre':
            return False
        hT = bufs2[hp]
        oT = bufs2[1 - hp]
        BhTl = BhT[hp]
        BoTl = BoT[1 - hp]
        vb = li * 8
        S.barrier()
        gbase = li * NBLK

        o3 = v3(oT, 16)
        cqn = oT[:, 12 * 2048:16 * 2048]
        ckvn = oT[:, 0:2 * 2048]
        kpe = oT[:, 2 * 2048:3 * 2048]
        qpe = [oT[:, 3 * 2048:4 * 2048], oT[:, 4 * 2048:5 * 2048]]
        wuq_sb = oT[:, 5 * 2048:5 * 2048 + 3072]
        wukv_sb = oT[:, 7 * 2048:8 * 2048]
        Bcqn = [Buf("cqn%d" % c) for c in range(4)]
        Bckvn = [Buf("ckvn%d" % c) for c in range(4)]
        Bkpe = Buf("kpe")
        Bqpe = [[Buf("qpe%d_%d" % (i, c)) for c in range(4)] for i in range(2)]
        Bwu = Buf("wu")
        alias_all = [BoTl[m][c] for m in list(range(0, 8)) + list(range(12, 16)) for c in range(4)]
        S.dma('pool', lambda e: e.dma_start(out=wuq_sb, in_=wuq_d[li * 128:(li + 1) * 128, :]), 'wu', writes=[Bwu] + alias_all)
        S.dma('pool', lambda e: e.dma_start(out=wukv_sb, in_=wukv_d[li * 128:(li + 1) * 128, :]), 'wu', writes=[Bwu])
        wuq3 = v3(wuq_sb, 4)
        wukv3 = v3(wukv_sb, 2)

        def lowrank_norm(gi0, nblk, dstbuf, Bdst, gcol, n):
            d3 = v3(dstbuf, nblk)
            for blk in range(nblk):
                def consume(tc, bk, blk=blk):
                    i2 = tc % 2
                    S.op('act', lambda e: e.activation(out=sqs[i2], in_=bank(bk), func=AF.Square), reads=[Bbank[bk]], writes=[Bsq[i2]])
                    S.op('dve', lambda e: e.tensor_copy(out=d3[:, blk, tc * 512:(tc + 1) * 512], in_=bank(bk)), reads=[Bbank[bk]], writes=[Bdst[tc]])
                    S.op('pe', lambda e: e.matmul(bank(tc), ones, sqs[i2], start=(blk == 0), stop=(blk == nblk - 1)),
                         reads=[Bsq[i2], Bconst], writes=[Bbank[tc]])
                proj_block_fm(hT, BhTl, gi0 + blk, 128, consume)
            flush_pend()
            for tc in range(4):
                i2 = tc % 2
                rstd_from_bank(tc, 128, rsb[i2], Brs[i2], float(n))
                for blk in range(nblk):
                    S.op('dve', lambda e, blk=blk, tc=tc, i2=i2: e.scalar_tensor_tensor(
                        out=d3[:, blk, tc * 512:(tc + 1) * 512], in0=d3[:, blk, tc * 512:(tc + 1) * 512],
                        scalar=vecs[:, vb + gcol + blk: vb + gcol + blk + 1], in1=rsb[i2], op0=ALU.mult, op1=ALU.mult),
                        reads=[Bdst[tc], Brs[i2], Bconst], writes=[Bdst[tc]])

        lowrank_norm(gbase + 0, 4, cqn, Bcqn, 2, 512)
        if stop_after == 'B1':
            return False
        lowrank_norm(gbase + 4, 2, ckvn, Bckvn, 6, 256)
        proj_block_fm(hT, BhTl, gbase + 6, 64,
                      lambda tc, bk: rope_chunk(bk, 64, 32, tc, TBc, TBs, permB, None, 0, kpe[0:64, tc * 512:(tc + 1) * 512], Bkpe, tc % 2))

        def prepB(h, slot):
            proj_block_fm(hT, BhTl, gbase + 7 + h, 128,
                          lambda tc, bk: silu_chunk(bk, gT[slot][:, tc * 512:(tc + 1) * 512], BgT[slot][tc], tc % 2))
            for tc in range(4):
                bk = next_pp()
                fns = [lambda e, kc=kc, bk=bk, tc=tc: e.matmul(bank(bk), wuq3[:, kc, h * 192:h * 192 + 128], v3(cqn, 4)[:, kc, tc * 512:(tc + 1) * 512],
                                                               start=(kc == 0), stop=(kc == 3)) for kc in range(4)]
                S.op('pe', fns, reads=[Bwu, Bcqn[tc]], writes=[Bbank[bk]])
                flush_pend()
                S.op('act', lambda e, bk=bk, tc=tc: e.activation(out=qT[slot][:, tc * 512:(tc + 1) * 512], in_=bank(bk), func=AF.Copy),
                     reads=[Bbank[bk]], writes=[BqT[slot][tc]])
                bk = next_pp()
                fns = [lambda e, kc=kc, bk=bk, tc=tc: e.matmul(bank(bk)[0:64, :], wuq3[:, kc, h * 192 + 128:h * 192 + 192], v3(cqn, 4)[:, kc, tc * 512:(tc + 1) * 512],
                                                               start=(kc == 0), stop=(kc == 3)) for kc in range(4)]
                S.op('pe', fns, reads=[Bwu, Bcqn[tc]], writes=[Bbank[bk]])
                rope_chunk(bk, 64, 32, tc, TBc, TBs, permB, None, 0, qpe[slot][0:64, tc * 512:(tc + 1) * 512], Bqpe[slot][tc], tc % 2)

        def prepB_kv(h):
            for tc in range(4):
                bk = next_pp()
                fns = [lambda e, kc=kc, bk=bk, tc=tc: e.matmul(bank(bk), wukv3[:, kc, h * 256:h * 256 + 128], v3(ckvn, 2)[:, kc, tc * 512:(tc + 1) * 512],
                                                               start=(kc == 0), stop=(kc == 1)) for kc in range(2)]
                S.op('pe', fns, reads=[Bwu, Bckvn[tc]], writes=[Bbank[bk]])
                flush_pend()
                S.op('act', lambda e, bk=bk, tc=tc: e.activation(out=kT[:, tc * 512:(tc + 1) * 512], in_=bank(bk), func=AF.Copy),
                     reads=[Bbank[bk]], writes=[BkT[tc]])
            for tq in range(4):
                bk = next_pp()
                fns = []
                for j in range(4):
                    tt = tq * 4 + j
                    for kc in range(2):
                        fns.append(lambda e, kc=kc, tt=tt, j=j, bk=bk: e.matmul(bank(bk)[:, j * 128:(j + 1) * 128], v3(ckvn, 2)[:, kc, tt * 128:(tt + 1) * 128],
                                                                                wukv3[:, kc, h * 256 + 128:h * 256 + 256], start=(kc == 0), stop=(kc == 1)))
                S.op('pe', fns, reads=[Bwu, Bckvn[tq]], writes=[Bbank[bk]])
                S.op('act', lambda e, bk=bk, tq=tq: e.activation(out=Vt[:, tq * 512:(tq + 1) * 512], in_=bank(bk), func=AF.Copy),
                     reads=[Bbank[bk]], writes=[BVt[tq]])

        if stop_after == 'B2':
            return False
        scaleB = float(192 ** -0.5)
        prepB(0, 0)
        if stop_after == 'B3':
            return False
        if stop_after == 'B4':
            prepB_kv(0)
            return False
        for h in range(4):
            if h + 1 < 4:
                prepB(h + 1, (h + 1) % 2)
            prepB_kv(h)
            attn_dense(oT, BoTl, 8 + h, h % 2, h % 2, scaleB, kpe=kpe, qpe=qpe[h % 2], Bkpe=Bkpe, Bqpe=Bqpe[h % 2])
        if stop_after == 'B':
            return False

        Balias = Bcqn + Bckvn + [Bkpe, Bwu] + Bqpe[0] + Bqpe[1]

        def alias_guard(mc):
            for b in Balias:
                for c in range(4):
                    for k, v in b.r.items():
                        if BoTl[mc][c].r.get(k, 0) < v:
                            BoTl[mc][c].r[k] = v
                    if b.w is not None:
                        k, v = b.w
                        if BoTl[mc][c].r.get(k, 0) < v:
                            BoTl[mc][c].r[k] = v
        for mc in list(range(0, 8)) + list(range(12, 16)):
            alias_guard(mc)

        S.dma('sp', lambda e: e.dma_start(out=cmask, in_=cmask_d), 'cmask', writes=[Bcmask])
        scaleC = float(128 ** -0.5)

        def prepC_bias(h, bslot):
            S.dma('pool', lambda e: e.dma_start(out=biasm[bslot], in_=cbias_d[(li * 4 + h) * 128:(li * 4 + h + 1) * 128, :]), 'bias%d' % bslot, writes=[Bbiasm[bslot]])
            S.op('pool', lambda e: e.tensor_tensor(out=biasm[bslot], in0=biasm[bslot], in1=cmask, op=ALU.add), reads=[Bbiasm[bslot], Bcmask], writes=[Bbiasm[bslot]])

        def prepC_q(h, slot):
            g0 = gbase + 11 + 4 * h
            proj_block_fm(hT, BhTl, g0, 128,
                          lambda tc, bk: S.op('act', lambda e: e.activation(out=qT[slot][:, tc * 512:(tc + 1) * 512], in_=bank(bk), func=AF.Copy, scale=scaleC),
                                              reads=[Bbank[bk]], writes=[BqT[slot][tc]]))

        def prepC_kvg(h, slot):
            g0 = gbase + 11 + 4 * h
            proj_block_fm(hT, BhTl, g0 + 1, 128,
                          lambda tc, bk: S.op('dve', lambda e: e.tensor_copy(out=kT[:, tc * 512:(tc + 1) * 512], in_=bank(bk)),
                                              reads=[Bbank[bk]], writes=[BkT[tc]]))
            proj_block_tm(hT, BhTl, g0 + 2, Vt, BVt)
            proj_block_fm(hT, BhTl, g0 + 3, 128,
                          lambda tc, bk: silu_chunk(bk, gT[slot][:, tc * 512:(tc + 1) * 512], BgT[slot][tc], tc % 2))

        for h in range(4):
            prepC_bias(h, h % 2)
            prepC_q(h, h % 2)
            if stop_after == 'C1':
                break
            prepC_kvg(h, h % 2)
            if stop_after == 'C2':
                break
            attn_window(oT, BoTl, 12 + h, h % 2, h % 2, h % 2)
            if stop_after == 'C3':
                break
        if stop_after in ('C1', 'C2', 'C3'):
            return False
        if stop_after == 'C':
            return False

        scaleA = float(128 ** 0.5)
        aq = vecs[:, vb + 0:vb + 1]
        ak = vecs[:, vb + 1:vb + 2]

        def prepA_kv(g):
            g0 = gbase + 27 + 10 * g
            proj_block_fm(hT, BhTl, g0, 128,
                          lambda tc, bk: rope_chunk(bk, 128, 64, tc, TAc, TAs, permA, ak, 128.0, kT[:, tc * 512:(tc + 1) * 512], BkT[tc], tc % 2))
            proj_block_tm(hT, BhTl, g0 + 1, Vt, BVt)

        def prepA_q(h, slot):
            g = h // 4
            g0 = gbase + 27 + 10 * g + 2 + 2 * (h % 4)
            proj_block_fm(hT, BhTl, g0, 128,
                          lambda tc, bk: rope_chunk(bk, 128, 64, tc, TAc, TAs, permA, aq, 128.0, qT[slot][:, tc * 512:(tc + 1) * 512], BqT[slot][tc], tc % 2))
            proj_block_fm(hT, BhTl, g0 + 1, 128,
                          lambda tc, bk: silu_chunk(bk, gT[slot][:, tc * 512:(tc + 1) * 512], BgT[slot][tc], tc % 2))

        w3o = v3(hT, 16)
        wout_issued = {'v': False}

        def issue_wout():
            wout_issued['v'] = True
            for mc in range(16):
                S.dma('pool', lambda e, mc=mc: e.dma_start(out=w3o[:, mc, :], in_=wout_d[(li * 16 + mc) * 128:(li * 16 + mc + 1) * 128, :]),
                      'wout', writes=BhTl if mc == 0 else [])

        for g in range(2):
            prepA_kv(g)
            prepA_q(4 * g, 0)
            for hh in range(4):
                h = 4 * g + hh
                if hh + 1 < 4:
                    prepA_q(h + 1, (hh + 1) % 2)
                    if h + 1 == 7:
                        issue_wout()
                attn_dense(oT, BoTl, h, hh % 2, hh % 2, scaleA)
        if stop_after == 'A':
            return False

        flush_pend()
        if not wout_issued['v']:
            issue_wout()
        Bwout = Buf("wout")
        Bwout.w = ('wout', S.dcnt['wout'])
        S.barrier()
        last = (li == L - 1)
        load_gains(li + 1 if not last else li, True) if not last else \
            S.dma('sp', lambda e: e.dma_start(out=gpost_bc, in_=gpost_d[li:li + 1, :].broadcast_to([128, D])), 'gpost', writes=[Bgpost])
        src_d = x_d if li == 0 else out_d
        o3 = v3(oT, 16)
        ybank = {'i': 0}
        Bjunk = Buf("junk")
        pend1 = None
        pend2 = None
        Bot = [Buf("ot%d" % t) for t in range(16)]
        for tt in range(16):
            i2 = tt % 2
            tcq = tt // 4
            S.dma('sp', lambda e, tt=tt, i2=i2: e.dma_start(out=xt[i2], in_=src_d[tt * 128:(tt + 1) * 128, :]), 'xt%d' % i2,
                  reads=[Bout[tt]], writes=[Bxt[i2]])
            ssp = stat[:, 16 + 8 * i2:16 + 8 * i2 + 4]
            for nb in range(4):
                bk = ybank['i'] % 6
                ybank['i'] += 1
                fns = [lambda e, mc=mc, nb=nb, bk=bk, tt=tt: e.matmul(bank(bk), o3[:, mc, tt * 128:(tt + 1) * 128], w3o[:, mc, nb * 512:(nb + 1) * 512],
                                                                      start=(mc == 0), stop=(mc == 15)) for mc in range(16)]
                S.op('pe', fns, reads=[Bwout, Bot[tt]], writes=[Bbank[bk]])
                S.op('act', lambda e, nb=nb, bk=bk, i2=i2, ssp=ssp: e.activation(out=junk, in_=bank(bk), func=AF.Square, accum_out=ssp[:, nb:nb + 1]),
                     reads=[Bbank[bk]], writes=[Bjunk, Bst_y[i2]])
                S.op('dve', lambda e, nb=nb, bk=bk, i2=i2: e.tensor_copy(out=yraw[i2][:, nb * 512:(nb + 1) * 512], in_=bank(bk)),
                     reads=[Bbank[bk]], writes=[Byraw[i2]])
            ss = stat[:, 32 + 4 * i2:32 + 4 * i2 + 1]
            rr = stat[:, 32 + 4 * i2 + 1:32 + 4 * i2 + 2]
            S.op('dve', lambda e, ss=ss, ssp=ssp: e.tensor_reduce(out=ss, in_=ssp, axis=mybir.AxisListType.X, op=ALU.add), reads=[Bst_y[i2]], writes=[Bst_y[i2]])
            S.op('act', lambda e, ss=ss, rr=rr: e.activation(out=rr, in_=ss, func=AF.Ln, scale=1.0 / D, bias=EPS), reads=[Bst_y[i2]], writes=[Bst_y[i2]])
            S.op('act', lambda e, rr=rr: e.activation(out=rr, in_=rr, func=AF.Exp, scale=-0.5), reads=[Bst_y[i2]], writes=[Bst_y[i2]])
            S.op('dve', lambda e, i2=i2, rr=rr: e.scalar_tensor_tensor(out=yraw[i2], in0=yraw[i2], scalar=rr, in1=gpost_bc, op0=ALU.mult, op1=ALU.mult),
                 reads=[Byraw[i2], Bst_y[i2], Bgpost], writes=[Byraw[i2]])
            S.op('pool', lambda e, i2=i2: e.tensor_tensor(out=xt[i2], in0=xt[i2], in1=yraw[i2], op=ALU.add), reads=[Bxt[i2], Byraw[i2]], writes=[Bxt[i2]])
            S.dma('sp', lambda e, tt=tt, i2=i2: e.dma_start(out=out_d[tt * 128:(tt + 1) * 128, :], in_=xt[i2]), 'st%d' % i2,
                  reads=[Bxt[i2]], writes=[Bout[tt]])
            if not last:
                if pend2 is not None:
                    hT_part2(pend2, oT, [BhT[1 - hp][pend2], Bot[pend2]], pend2 % 2)
                    pend2 = None
                if pend1 is not None:
                    hT_part1(xt[pend1 % 2], Bxt[pend1 % 2], pend1 % 2, 4 * (pend1 % 2))
                    pend2 = pend1
                pend1 = tt
        if not last:
            if pend2 is not None:
                hT_part2(pend2, oT, [BhT[1 - hp][pend2], Bot[pend2]], pend2 % 2)
            hT_part1(xt[pend1 % 2], Bxt[pend1 % 2], pend1 % 2, 4 * (pend1 % 2))
            hT_part2(pend1, oT, [BhT[1 - hp][pend1], Bot[pend1]], pend1 % 2)
        return True


    for li in range(L):
        if not do_layer(li, hp):
            break
        hp = 1 - hp

    if 'oT' in dbg_d:
        S.barrier()
        src = bufs2[1]
        for kc in dbg.get('oT_chunks', range(16)):
            S.op('dve', lambda e, kc=kc: e.tensor_copy(out=yraw[0], in_=v3(src, 16)[:, kc, :]), reads=[], writes=[Byraw[0]])
            S.dma('sp', lambda e, kc=kc: e.dma_start(out=dbg_d['oT'][kc * 128:(kc + 1) * 128, :], in_=yraw[0]), 'dbg', reads=[Byraw[0]])
    S.barrier()
    S.emit(nc)
    st.close()
    return nc


def _prep_shared(norm_pre, norm_post, w_in, a_q_norm, a_k_norm, b_q_norm, b_kv_norm, b_w_uq, b_w_ukv, c_rpb, w_out):
    L = w_in.shape[0]
    cols = _block_cols()
    win = np.empty((L, NBLK, 128, 16, 128), np.float32)
    for l in range(L):
        g = w_in[l][:, cols.reshape(-1)].reshape(16, 128, NBLK, 128)
        win[l] = g.transpose(2, 1, 0, 3)
    win = win.reshape(L * NBLK * 128, 2048)
    wuq = np.ascontiguousarray(b_w_uq.reshape(L, 4, 128, 768).transpose(0, 2, 1, 3)).reshape(L * 128, 4 * 768)
    wukv = np.ascontiguousarray(b_w_ukv.reshape(L, 2, 128, 1024).transpose(0, 2, 1, 3)).reshape(L * 128, 2 * 1024)
    wout = np.ascontiguousarray(w_out.reshape(L * 16 * 128, 2048))
    vecs = np.zeros((128, L * 8), np.float32)
    for l in range(L):
        vecs[:, l * 8 + 0] = a_q_norm[l]
        vecs[:, l * 8 + 1] = a_k_norm[l]
        vecs[:, l * 8 + 2:l * 8 + 6] = b_q_norm[l].reshape(4, 128).T
        vecs[:, l * 8 + 6:l * 8 + 8] = b_kv_norm[l].reshape(2, 128).T
    dr, dc, va = _c_bias_index()
    cb = c_rpb[:, :, dr, dc]
    cb = np.where(va[None, None], cb, np.float32(0.0))
    cbias = np.ascontiguousarray(cb.transpose(0, 1, 3, 2, 4)).reshape(L * 4 * 128, 21 * 128).astype(np.float32)
    cmats, tabs, cmask = _const_tables()
    return {"win": win, "wuq": wuq, "wukv": wukv, "wout": wout,
            "gpre": np.ascontiguousarray(norm_pre, dtype=np.float32), "gpost": np.ascontiguousarray(norm_post, dtype=np.float32),
            "vecs": vecs, "cbias": cbias, "cmask": cmask, "cmats": cmats, "ctabs": tabs}


def kernel(x, norm_pre, norm_post, w_in, a_q_norm, a_k_norm, b_q_norm, b_kv_norm, b_w_uq, b_w_ukv, c_rpb, w_out):
    args = [np.asarray(a, dtype=np.float32) for a in (norm_pre, norm_post, w_in, a_q_norm, a_k_norm, b_q_norm, b_kv_norm, b_w_uq, b_w_ukv, c_rpb, w_out)]
    x = np.asarray(x, dtype=np.float32)
    shared = _prep_shared(*args)
    nc = bass.Bass("TRN2", target_bir_lowering=False)
    build(nc, L=DEPTH)
    n = x.shape[0]
    in_maps = []
    for b in range(n):
        m = dict(shared)
        m["x"] = np.ascontiguousarray(x[b])
        in_maps.append(m)
    res = run_bass_kernel_spmd(nc, in_maps, core_ids=list(range(n)))
    return np.stack([np.asarray(r["out"], dtype=np.float32) for r in res.results], axis=0)
```

```python
import contextlib
import numpy as np
import ml_dtypes
import concourse.bass as bass
import concourse.mybir as mybir
from concourse.bass_utils import run_bass_kernel_spmd

F32 = mybir.dt.float32
BF16 = mybir.dt.bfloat16
U8 = mybir.dt.uint8
AF = mybir.ActivationFunctionType
ALU = mybir.AluOpType
ENGS = ('pe', 'act', 'dve', 'pool', 'sp')

S_TOK = 2048
D = 2048
DEPTH = 2
NBLK = 47
EPS = 1e-6
NEG = -80.0


class Buf:
    __slots__ = ('name', 'w', 'r', 'excl')

    def __init__(s, name, excl=False):
        s.name = name
        s.w = None
        s.r = {}
        s.excl = excl


class Sched:
    def __init__(s):
        s.q = {e: [] for e in ENGS}
        s.cnt = {e: 0 for e in ENGS}
        s.seen = {e: {} for e in ENGS}
        s.dcnt = {}

    def _waits(s, eng, reads, writes):
        need = {}

        def add(tok):
            if tok is None:
                return
            k, v = tok
            if need.get(k, 0) < v:
                need[k] = v
        for b in reads:
            add(b.w)
            if b.excl:
                for k, v in b.r.items():
                    if k != eng:
                        add((k, v))
        for b in writes:
            if b.w is not None and not (eng == 'pe' and b.w[0] == 'pe'):
                add(b.w)
            for k, v in b.r.items():
                if k != eng:
                    add((k, v))
        out = []
        for k, v in need.items():
            if s.seen[eng].get(k, 0) < v:
                s.seen[eng][k] = v
                out.append((k, v))
        return out

    def _mark(s, tok, reads, writes):
        for b in reads:
            if b.r.get(tok[0], 0) < tok[1]:
                b.r[tok[0]] = tok[1]
        for b in writes:
            b.w = tok
            b.r = {}

    def op(s, eng, fns, reads=(), writes=(), fuse=None):
        if not isinstance(fns, (list, tuple)):
            fns = [fns]
        if fuse is None:
            fuse = eng in ('act', 'dve', 'pool')
        waits = s._waits(eng, reads, writes)
        s.cnt[eng] += 1
        tok = (eng, s.cnt[eng])
        s.q[eng].append((waits, list(fns), (eng, 1), fuse))
        s._mark(tok, reads, writes)
        return tok

    def dma(s, eng, fn, key, reads=(), writes=()):
        waits = s._waits(eng, reads, writes)
        s.dcnt[key] = s.dcnt.get(key, 0) + 16
        tok = (key, s.dcnt[key])
        s.q[eng].append((waits, [fn], (key, 16), False))
        s._mark(tok, reads, writes)
        return tok

    def barrier(s):
        toks = [(e, s.cnt[e]) for e in ENGS if s.cnt[e] > 0] + list(s.dcnt.items())
        for e in ENGS:
            waits = []
            for k, v in toks:
                if k == e:
                    continue
                if s.seen[e].get(k, 0) < v:
                    s.seen[e][k] = v
                    waits.append((k, v))
            if waits:
                s.q[e].append((waits, [], None, False))

    def emit(s, nc):
        keys = list(ENGS) + list(s.dcnt.keys())
        with contextlib.ExitStack() as st:
            sems = {k: st.enter_context(nc.semaphore("s_" + str(k))) for k in keys}
            block = st.enter_context(nc.Block())

            def run(e, name):
                for waits, fns, inc, fuse in s.q[name]:
                    att = None
                    if fuse and waits and fns:
                        att = waits[-1]
                        waits = waits[:-1]
                    for k, v in waits:
                        e.wait_ge(sems[k], v)
                    ins = None
                    for i, fn in enumerate(fns):
                        ins = fn(e)
                        if i == 0 and att is not None:
                            ins._wait_ge(sems[att[0]], att[1])
                    if inc is not None and ins is not None:
                        ins.then_inc(sems[inc[0]], inc[1])

            @block.tensor
            def _(e):
                run(e, 'pe')

            @block.scalar
            def _(e):
                run(e, 'act')

            @block.vector
            def _(e):
                run(e, 'dve')

            @block.gpsimd
            def _(e):
                run(e, 'pool')

            @block.sync
            def _(e):
                run(e, 'sp')


def _block_cols():
    blks = []
    for i in range(4):
        blks.append(list(range(2560 + 128 * i, 2560 + 128 * (i + 1))))
    for i in range(2):
        blks.append(list(range(3072 + 128 * i, 3072 + 128 * (i + 1))))
    blks.append(list(range(3328, 3392)) + list(range(3328, 3392)))
    for h in range(4):
        blks.append(list(range(3392 + 128 * h, 3392 + 128 * (h + 1))))
    for h in range(4):
        for base in (3904, 4416, 4928, 5440):
            blks.append(list(range(base + 128 * h, base + 128 * (h + 1))))
    for g in range(2):
        blks.append(list(range(1024 + 128 * g, 1024 + 128 * (g + 1))))
        blks.append(list(range(1280 + 128 * g, 1280 + 128 * (g + 1))))
        for h in range(4 * g, 4 * g + 4):
            blks.append(list(range(128 * h, 128 * (h + 1))))
            blks.append(list(range(1536 + 128 * h, 1536 + 128 * (h + 1))))
    assert len(blks) == NBLK
    return np.array(blks, dtype=np.int64)


def _c_slots():
    slots = []
    for u in range(16):
        if 2 <= u <= 13:
            slots.append([(u + d, d + 2) for d in range(-2, 3)])
        else:
            t0 = 0 if u < 2 else 12
            sp = {0: 0, 1: 1, 14: 2, 15: 3}[u]
            slots.append([(t0 + j, 5 + 4 * sp + j) for j in range(4)])
    return slots


def _c_bias_index():
    ent = []
    for d in range(-2, 3):
        ent.append((6, 6 + d))
    for u in (0, 1, 14, 15):
        t0 = 0 if u < 2 else 12
        for j in range(4):
            ent.append((u, t0 + j))
    dr = np.zeros((21, 128, 128), np.int64)
    dc = np.zeros((21, 128, 128), np.int64)
    va = np.zeros((21, 128, 128), bool)
    kl = np.arange(128)
    for e, (u, t) in enumerate(ent):
        q = u * 128 + kl
        k = t * 128 + kl
        qr, qc = q // 64, q % 64
        kr, kc = k // 64, k % 64
        r0 = np.clip(qr - 4, 0, 24)
        c0 = np.clip(qc - 8, 0, 48)
        ddr = kr[:, None] - qr[None, :]
        ddc = kc[:, None] - qc[None, :]
        ok = ((kr[:, None] >= r0[None, :]) & (kr[:, None] < r0[None, :] + 8) &
              (kc[:, None] >= c0[None, :]) & (kc[:, None] < c0[None, :] + 16))
        va[e] = ok
        dr[e] = np.where(ok, ddr + 7, 0)
        dc[e] = np.where(ok, ddc + 15, 0)
    return dr, dc, va


def _const_tables():
    ident = np.eye(128, dtype=np.float32)
    ones = np.ones((128, 128), np.float32)
    permA = np.zeros((128, 128), np.float32)
    for m in range(128):
        p = m + 32 if (m % 64) < 32 else m - 32
        permA[p, m] = 1.0
    permB = np.zeros((128, 128), np.float32)
    for m in range(64):
        p = m + 16 if (m % 32) < 16 else m - 16
        permB[p, m] = 1.0
    cmats = np.concatenate([ident, ones, permA, permB], axis=1).astype(ml_dtypes.bfloat16)
    theta = np.float32(10000.0)
    tabs = np.zeros((128, 4 * 64), np.float32)
    invA = (1.0 / (theta ** (np.arange(0, 64, 2, dtype=np.float32) / np.float32(64)))).astype(np.float32)
    invB = (1.0 / (theta ** (np.arange(0, 32, 2, dtype=np.float32) / np.float32(32)))).astype(np.float32)
    pos = np.arange(64, dtype=np.float32)
    for d in range(128):
        i = d % 32
        ang = (pos * invA[i]).astype(np.float32)
        sign = -1.0 if (d % 64) < 32 else 1.0
        tabs[d, 0:64] = np.cos(ang)
        tabs[d, 64:128] = sign * np.sin(ang)
    for d in range(64):
        i = d % 16
        ang = (pos * invB[i]).astype(np.float32)
        sign = -1.0 if (d % 32) < 16 else 1.0
        tabs[d, 128:192] = np.cos(ang)
        tabs[d, 192:256] = sign * np.sin(ang)
    _, _, va = _c_bias_index()
    cmask = np.where(va, 0.0, NEG).astype(np.float32)
    cmask = np.ascontiguousarray(cmask.transpose(1, 0, 2)).reshape(128, 21 * 128).astype(ml_dtypes.bfloat16)
    return cmats, tabs, cmask


def build(nc, L=DEPTH, dbg=None):
    dbg = dbg or {}
    stop_after = dbg.get('stop_after')

    def dram(name, shape, dt, kind):
        return nc.dram_tensor(name, shape, dt, kind=kind).ap()
    x_d = dram("x", [S_TOK, D], F32, "ExternalInput")
    win_d = dram("win", [L * NBLK * 128, 2048], F32, "ExternalInput")
    wuq_d = dram("wuq", [L * 128, 4 * 768], F32, "ExternalInput")
    wukv_d = dram("wukv", [L * 128, 2 * 1024], F32, "ExternalInput")
    wout_d = dram("wout", [L * 16 * 128, 2048], F32, "ExternalInput")
    gpre_d = dram("gpre", [L, D], F32, "ExternalInput")
    gpost_d = dram("gpost", [L, D], F32, "ExternalInput")
    vecs_d = dram("vecs", [128, L * 8], F32, "ExternalInput")
    cbias_d = dram("cbias", [L * 4 * 128, 21 * 128], F32, "ExternalInput")
    cmask_d = dram("cmask", [128, 21 * 128], BF16, "ExternalInput")
    cmats_d = dram("cmats", [128, 4 * 128], BF16, "ExternalInput")
    ctabs_d = dram("ctabs", [128, 4 * 64], F32, "ExternalInput")
    out_d = dram("out", [S_TOK, D], F32, "ExternalOutput")
    dbg_d = {}
    for name, shape in dbg.get('outs', {}).items():
        dbg_d[name] = dram(name, shape, F32, "ExternalOutput")

    S = Sched()
    st = contextlib.ExitStack()
    ARENA = 211968
    arena = st.enter_context(nc.sbuf_tensor("arena", [128, ARENA], U8))
    psum = st.enter_context(nc.psum_tensor("psum", [128, 4096], F32))
    psum_bf = psum[:, :].bitcast(BF16)

    def carve(off, nbytes, dt):
        assert off + nbytes <= ARENA, (off, nbytes)
        return arena[:, off:off + nbytes].bitcast(dt)

    bufs2 = [carve(0, 65536, BF16), carve(65536, 65536, BF16)]
    off = 131072
    cm = carve(off, 1024, BF16); off += 1024
    ident, ones, permA, permB = (cm[:, i * 128:(i + 1) * 128] for i in range(4))
    tabs = carve(off, 1024, F32); off += 1024
    vecs = carve(off, 64, F32); off += 64
    stat = carve(off, 256, F32); off += 256
    off = 134144
    NW = 3
    wsl = [carve(off + i * 4096, 4096, BF16) for i in range(NW)]; off += NW * 4096
    OV = off
    kT = carve(off, 4096, BF16); off += 4096
    Vt = carve(off, 4096, BF16); off += 4096
    qT = [carve(off + i * 4096, 4096, BF16) for i in range(2)]; off += 8192
    gT = [carve(off + i * 4096, 4096, BF16) for i in range(2)]; off += 8192
    Psb = [carve(off + i * 2048, 2048, BF16) for i in range(2)]; off += 4096
    sqs = [carve(off + i * 1024, 1024, BF16) for i in range(2)]; off += 2048
    qgs = [carve(off + i * 1024, 1024, BF16) for i in range(2)]; off += 2048
    rsb = [carve(off + i * 2048, 2048, F32) for i in range(2)]; off += 4096
    t1s = [carve(off + i * 2048, 2048, F32) for i in range(2)]; off += 4096
    t2s = [carve(off + i * 2048, 2048, F32) for i in range(2)]; off += 4096
    rden = carve(off, 2048, F32); off += 2048
    otmp = carve(off, 2048, F32); off += 2048
    cmask = carve(off, 5376, BF16); off += 5376
    biasm = [carve(off + i * 5376, 5376, BF16) for i in range(2)]; off += 10752
    assert off <= ARENA, off
    off = OV
    gpre_bc = carve(off, 8192, F32); off += 8192
    gpost_bc = carve(off, 8192, F32); off += 8192
    xt = [carve(off + i * 8192, 8192, F32) for i in range(2)]; off += 16384
    yraw = [carve(off + i * 8192, 8192, F32) for i in range(2)]; off += 16384
    xn = [carve(off + i * 4096, 4096, BF16) for i in range(2)]; off += 8192
    junk = carve(off, 1024, BF16); off += 1024
    assert off <= ARENA

    def bank(i, n=1):
        return psum[:, i * 512:(i + n) * 512]
    Bbank = [Buf("bank%d" % i, excl=True) for i in range(8)]

    Bconst = Buf("const")
    BhT = [[Buf("hT%d_%d" % (p, t)) for t in range(16)] for p in range(2)]
    BoT = [[[Buf("oT%d_%d_%d" % (p, m, c)) for c in range(4)] for m in range(16)] for p in range(2)]
    Bw = [Buf("w%d" % i) for i in range(NW)]
    BkT = [Buf("kT%d" % c) for c in range(4)]
    BVt = [Buf("Vt%d" % c) for c in range(4)]
    BqT = [[Buf("qT%d_%d" % (i, c)) for c in range(4)] for i in range(2)]
    BgT = [[Buf("gT%d_%d" % (i, c)) for c in range(4)] for i in range(2)]
    BP = [Buf("P%d" % i) for i in range(2)]
    Bsq = [Buf("sq%d" % i) for i in range(2)]
    Bqg = [Buf("qg%d" % i) for i in range(2)]
    Brs = [Buf("rs%d" % i) for i in range(2)]
    Bt1 = [Buf("t1%d" % i) for i in range(2)]
    Bt2 = [Buf("t2%d" % i) for i in range(2)]
    Brden = Buf("rden"); Botmp = Buf("otmp")
    Bcmask = Buf("cmask"); Bbiasm = [Buf("biasm%d" % i) for i in range(2)]
    Bgpre = Buf("gpre"); Bgpost = Buf("gpost")
    Bxt = [Buf("xt%d" % i) for i in range(2)]
    Byraw = [Buf("yraw%d" % i) for i in range(2)]
    Bxn = [Buf("xn%d" % i) for i in range(2)]
    Bst_h = [Buf("sth%d" % i) for i in range(2)]
    Bst_y = [Buf("sty%d" % i) for i in range(2)]
    Bout = [Buf("outrow%d" % t) for t in range(16)]

    S.dma('sp', lambda e: e.dma_start(out=cm, in_=cmats_d), 'c0', writes=[Bconst])
    S.dma('sp', lambda e: e.dma_start(out=tabs, in_=ctabs_d), 'c1', writes=[Bconst])
    S.dma('sp', lambda e: e.dma_start(out=vecs[:, 0:L * 8], in_=vecs_d), 'c2', writes=[Bconst])
    TAc, TAs, TBc, TBs = (tabs[:, i * 64:(i + 1) * 64] for i in range(4))
    S.op('pool', lambda e: e.memset(stat[:, 63:64], -1.0), reads=[], writes=[Bconst])

    def v3(ap, k):
        return ap.rearrange("p (k n) -> p k n", k=k)

    wstate = {'issued': 0}

    def w_issue_upto(gi):
        while wstate['issued'] <= gi and wstate['issued'] < L * NBLK:
            i = wstate['issued']
            sl = i % NW
            S.dma('pool', lambda e, i=i, sl=sl: e.dma_start(out=wsl[sl], in_=win_d[i * 128:(i + 1) * 128, :]),
                  'w%d' % sl, writes=[Bw[sl]])
            wstate['issued'] += 1

    def wget(gi):
        w_issue_upto(gi + NW - 1)
        sl = gi % NW
        return v3(wsl[sl], 16), Bw[sl]

    pp_state = {'i': 0}

    def next_pp():
        i = pp_state['i']
        pp_state['i'] ^= 1
        return 6 + i

    def proj_fm(hT, BhTl, w3, Bwb, M, tc, bk):
        h3 = v3(hT, 16)
        fns = []
        for kc in range(16):
            fns.append(lambda e, kc=kc: e.matmul(bank(bk)[0:M, :], w3[:, kc, 0:M], h3[:, kc, tc * 512:(tc + 1) * 512],
                                                 start=(kc == 0), stop=(kc == 15)))
        S.op('pe', fns, reads=[Bwb] + BhTl[tc * 4:(tc + 1) * 4], writes=[Bbank[bk]])

    def rstd_from_bank(bk, P, r_ap, Br, n, reads_extra=()):
        S.op('act', lambda e: e.activation(out=r_ap, in_=bank(bk)[0:P, :], func=AF.Ln, scale=1.0 / n, bias=EPS),
             reads=[Bbank[bk]] + list(reads_extra), writes=[Br])
        S.op('act', lambda e: e.activation(out=r_ap, in_=r_ap, func=AF.Exp, scale=-0.5), reads=[Br], writes=[Br])

    def rope_chunk(src_bk, P, half, tc, cosT, sinT, perm, g_ap, norm_n, dst_ap, Bdst, i2, scale_mult=None):
        sq, qg, r, t1, t2 = sqs[i2][0:P, :], qgs[i2][0:P, :], rsb[i2][0:P, :], t1s[i2][0:P, :], t2s[i2][0:P, :]
        ss_bk, sw_bk = (0, 2) if i2 == 0 else (1, 3)
        src = bank(src_bk)[0:P, :]
        if norm_n:
            S.op('act', lambda e: e.activation(out=sq, in_=src, func=AF.Square), reads=[Bbank[src_bk]], writes=[Bsq[i2]])
        if g_ap is not None:
            S.op('dve', lambda e: e.tensor_scalar(out=qg, in0=src, scalar1=g_ap, scalar2=None, op0=ALU.mult),
                 reads=[Bbank[src_bk], Bconst], writes=[Bqg[i2]])
        else:
            S.op('dve', lambda e: e.tensor_copy(out=qg, in_=src), reads=[Bbank[src_bk]], writes=[Bqg[i2]])
        if norm_n:
            S.op('pe', lambda e: e.matmul(bank(ss_bk)[0:P, :], ones[0:P, 0:P], sq, start=True, stop=True),
                 reads=[Bsq[i2], Bconst], writes=[Bbank[ss_bk]])
        S.op('pe', lambda e: e.matmul(bank(sw_bk)[0:P, :], perm[0:P, 0:P], qg, start=True, stop=True),
             reads=[Bqg[i2], Bconst], writes=[Bbank[sw_bk]])
        if norm_n:
            S.op('act', lambda e: e.activation(out=r, in_=bank(ss_bk)[0:P, :], func=AF.Ln, scale=1.0, bias=norm_n * EPS),
                 reads=[Bbank[ss_bk]], writes=[Brs[i2]])
            S.op('act', lambda e: e.activation(out=r, in_=r, func=AF.Exp, scale=-0.5), reads=[Brs[i2]], writes=[Brs[i2]])
        sw = bank(sw_bk)
        lo, hi = slice(0, half), slice(half, P)

        def rowb(tab):
            return tab[lo, tc * 8:(tc + 1) * 8].unsqueeze(2).broadcast_to([half, 8, 64])

        def colb(tab):
            return tab[hi, 0:64].unsqueeze(1).broadcast_to([P - half, 8, 64])

        def v8(ap):
            return ap.rearrange("p (a b) -> p a b", a=8)
        S.op('pool', [lambda e: e.tensor_tensor(out=v8(t1s[i2][lo, :]), in0=v8(qgs[i2][lo, :]), in1=rowb(cosT), op=ALU.mult),
                      lambda e: e.tensor_tensor(out=v8(t1s[i2][hi, :]), in0=v8(qgs[i2][hi, :]), in1=colb(cosT), op=ALU.mult)],
             reads=[Bqg[i2], Bconst], writes=[Bt1[i2]])
        S.op('dve', [lambda e: e.tensor_tensor(out=v8(t2s[i2][lo, :]), in0=v8(sw[lo, :]), in1=rowb(sinT), op=ALU.mult),
                     lambda e: e.tensor_tensor(out=v8(t2s[i2][hi, :]), in0=v8(sw[hi, :]), in1=colb(sinT), op=ALU.mult)],
             reads=[Bbank[sw_bk], Bconst], writes=[Bt2[i2]])
        if norm_n:
            S.op('dve', lambda e: e.tensor_tensor(out=t1, in0=t1, in1=t2, op=ALU.add), reads=[Bt1[i2], Bt2[i2]], writes=[Bt1[i2]])
            S.op('dve', lambda e: e.tensor_tensor(out=dst_ap, in0=t1, in1=r, op=ALU.mult),
                 reads=[Bt1[i2], Brs[i2]], writes=[Bdst])
        else:
            S.op('dve', lambda e: e.tensor_tensor(out=dst_ap, in0=t1, in1=t2, op=ALU.add), reads=[Bt1[i2], Bt2[i2]], writes=[Bdst])

    def silu_chunk(src_bk, dst_ap, Bdst, i2):
        S.op('act', lambda e: e.activation(out=dst_ap, in_=bank(src_bk), func=AF.Silu), reads=[Bbank[src_bk]], writes=[Bdst])

    pend = {'f': None}

    def flush_pend():
        f = pend['f']
        pend['f'] = None
        if f is not None:
            f()

    def proj_block_fm(hT, BhTl, gi, M, consume):
        w3, Bwb = wget(gi)
        for tc in range(4):
            bk = next_pp()
            proj_fm(hT, BhTl, w3, Bwb, M, tc, bk)
            flush_pend()
            pend['f'] = (lambda tc=tc, bk=bk: consume(tc, bk))

    def proj_block_tm(hT, BhTl, gi, dst, Bdstl):
        w3, Bwb = wget(gi)
        h3 = v3(hT, 16)
        d3 = v3(dst, 16)
        for tq in range(4):
            bk = next_pp()
            fns = []
            for j in range(4):
                tt = tq * 4 + j
                for kc in range(16):
                    fns.append(lambda e, kc=kc, tt=tt, j=j, bk=bk: e.matmul(bank(bk)[:, j * 128:(j + 1) * 128], h3[:, kc, tt * 128:(tt + 1) * 128],
                                                                      w3[:, kc, :], start=(kc == 0), stop=(kc == 15)))
            S.op('pe', fns, reads=[Bwb] + BhTl[tq * 4:(tq + 1) * 4], writes=[Bbank[bk]])
            flush_pend()
            S.op('act', lambda e, tq=tq, bk=bk: e.activation(out=dst[:, tq * 512:(tq + 1) * 512], in_=bank(bk), func=AF.Copy),
                 reads=[Bbank[bk]], writes=[Bdstl[tq]])

    ep_pend = {'f': None}

    def flush_ep():
        f = ep_pend['f']
        ep_pend['f'] = None
        if f is not None:
            f()

    def attn_epilogue(oTbuf, BoTl, mc, qc, gslot, ob, db):
        o3 = v3(oTbuf, 16)
        S.op('act', lambda e: e.activation(out=rden, in_=bank(db), func=AF.Ln), reads=[Bbank[db]], writes=[Brden])
        S.op('act', lambda e: e.activation(out=rden, in_=rden, func=AF.Exp, scale=-1.0), reads=[Brden], writes=[Brden])
        S.op('dve', lambda e: e.tensor_tensor(out=otmp, in0=bank(ob), in1=rden, op=ALU.mult), reads=[Bbank[ob], Brden], writes=[Botmp])
        S.op('pool', lambda e: e.tensor_tensor(out=o3[:, mc, qc * 512:(qc + 1) * 512], in0=otmp, in1=gT[gslot][:, qc * 512:(qc + 1) * 512], op=ALU.mult),
             reads=[Botmp, BgT[gslot][qc]], writes=[BoTl[mc][qc]])

    def attn_dense(oTbuf, BoTl, mc, qslot, gslot, scale, kpe=None, qpe=None, Bkpe=None, Bqpe=None):
        q_ap = qT[qslot]
        seq = [(qc, kp) for qc in range(4) for kp in range(8)]

        def s_mm(idx):
            qc, kp = seq[idx]
            sb = idx % 2
            fns = []
            for j in range(2):
                kt = 2 * kp + j
                o_ = bank(2 * sb + j)
                if kpe is None:
                    fns.append(lambda e, o_=o_, kt=kt, qc=qc: e.matmul(o_, kT[:, kt * 128:(kt + 1) * 128], q_ap[:, qc * 512:(qc + 1) * 512], start=True, stop=True))
                else:
                    fns.append(lambda e, o_=o_, kt=kt, qc=qc: e.matmul(o_, kT[:, kt * 128:(kt + 1) * 128], q_ap[:, qc * 512:(qc + 1) * 512], start=True, stop=False))
                    fns.append(lambda e, o_=o_, kt=kt, qc=qc: e.matmul(o_, kpe[0:64, kt * 128:(kt + 1) * 128], qpe[0:64, qc * 512:(qc + 1) * 512], start=False, stop=True))
            rd = [BkT[kp // 2], BqT[qslot][qc]]
            if kpe is not None:
                rd += [Bkpe, Bqpe[qc]]
            S.op('pe', fns, reads=rd, writes=[Bbank[2 * sb], Bbank[2 * sb + 1]], fuse=(idx >= 2))

        def rest(idx):
            qc, kp = seq[idx]
            sb = idx % 2
            ob, db = (4, 5) if qc % 2 == 0 else (6, 7)
            S.op('act', lambda e: e.activation(out=Psb[sb], in_=bank(2 * sb, 2), func=AF.Exp, scale=scale),
                 reads=[Bbank[2 * sb], Bbank[2 * sb + 1]], writes=[BP[sb]], fuse=True)
            if kp == 2:
                flush_ep()
            S.op('dve', lambda e: e.tensor_tensor(out=sqs[sb], in0=Psb[sb][:, 0:512], in1=Psb[sb][:, 512:1024], op=ALU.add),
                 reads=[BP[sb]], writes=[Bsq[sb]], fuse=True)
            fns = []
            for j in range(2):
                kt = 2 * kp + j
                first = (kp == 0 and j == 0)
                last = (kp == 7 and j == 1)
                fns.append(lambda e, kt=kt, j=j, first=first, last=last: e.matmul(bank(ob), Vt[:, kt * 128:(kt + 1) * 128], Psb[sb][:, j * 512:(j + 1) * 512], start=first, stop=last))
            rd = [BP[sb], BVt[kp // 2], Bconst]
            wr = [Bbank[ob]]
            if kp > 0:
                fns.append(lambda e, kp=kp: e.matmul(bank(db), ones, sqs[1 - sb], start=(kp == 1), stop=False))
                rd.append(Bsq[1 - sb])
                wr.append(Bbank[db])
            S.op('pe', fns, reads=rd, writes=wr, fuse=(idx >= 1))
            if kp == 7:
                S.op('pe', lambda e: e.matmul(bank(db), ones, sqs[sb], start=False, stop=True), reads=[Bsq[sb], Bconst], writes=[Bbank[db]])
                ep_pend['f'] = (lambda qc=qc, ob=ob, db=db: attn_epilogue(oTbuf, BoTl, mc, qc, gslot, ob, db))
        s_mm(0)
        flush_pend()
        for idx in range(len(seq)):
            if idx + 1 < len(seq):
                s_mm(idx + 1)
            rest(idx)
        flush_ep()

    slots_c = _c_slots()

    def attn_window(oTbuf, BoTl, mc, qslot, gslot, bslot):
        q_ap = qT[qslot]
        bm = biasm[bslot]

        def s_mm(u):
            sb = u % 2
            fns = []
            for s_, (t, eidx) in enumerate(slots_c[u]):
                o_ = psum[:, sb * 1024 + s_ * 128: sb * 1024 + (s_ + 1) * 128]
                fns.append(lambda e, o_=o_, t=t, u=u: e.matmul(o_, kT[:, t * 128:(t + 1) * 128], q_ap[:, u * 128:(u + 1) * 128], start=True, stop=False))
                fns.append(lambda e, o_=o_, eidx=eidx: e.matmul(o_, ident, bm[:, eidx * 128:(eidx + 1) * 128], start=False, stop=True))
            ts = sorted(set(t // 4 for t, _ in slots_c[u]))
            S.op('pe', fns, reads=[BkT[c] for c in ts] + [BqT[qslot][u // 4], Bbiasm[bslot], Bconst], writes=[Bbank[2 * sb], Bbank[2 * sb + 1]], fuse=(u >= 4))

        def rest(u):
            sb = u % 2
            ns = len(slots_c[u])
            S.op('act', lambda e: e.activation(out=Psb[sb][:, 0:ns * 128], in_=psum[:, sb * 1024: sb * 1024 + ns * 128], func=AF.Exp, scale=1.0),
                 reads=[Bbank[2 * sb], Bbank[2 * sb + 1]], writes=[BP[sb]])
            uu = u % 4
            ob, db = (4, 5) if (u // 4) % 2 == 0 else (6, 7)
            fns = []
            for s_, (t, eidx) in enumerate(slots_c[u]):
                first = (s_ == 0)
                last = (s_ == ns - 1)
                fns.append(lambda e, t=t, s_=s_, first=first, last=last: e.matmul(bank(ob)[:, uu * 128:(uu + 1) * 128], Vt[:, t * 128:(t + 1) * 128], Psb[sb][:, s_ * 128:(s_ + 1) * 128], start=first, stop=last))
                fns.append(lambda e, s_=s_, first=first, last=last: e.matmul(bank(db)[:, uu * 128:(uu + 1) * 128], ones, Psb[sb][:, s_ * 128:(s_ + 1) * 128], start=first, stop=last))
            ts = sorted(set(t // 4 for t, _ in slots_c[u]))
            S.op('pe', fns, reads=[BP[sb], Bconst] + [BVt[c] for c in ts], writes=[Bbank[ob], Bbank[db]], fuse=(u >= 4))
            if uu == 3:
                attn_epilogue(oTbuf, BoTl, mc, u // 4, gslot, ob, db)
        s_mm(0)
        flush_pend()
        for u in range(16):
            if u + 1 < 16:
                s_mm(u + 1)
            rest(u)

    def hT_part1(src_ap, Bsrc, i2, scol):
        ss = stat[:, scol:scol + 1]
        rr = stat[:, scol + 1:scol + 2]
        S.op('act', lambda e: e.activation(out=xn[i2], in_=src_ap, func=AF.Square, accum_out=ss), reads=[Bsrc], writes=[Bxn[i2], Bst_h[i2]], fuse=False)
        S.op('act', lambda e: e.activation(out=rr, in_=ss, func=AF.Ln, scale=1.0 / D, bias=EPS), reads=[Bst_h[i2]], writes=[Bst_h[i2]])
        S.op('act', lambda e: e.activation(out=rr, in_=rr, func=AF.Exp, scale=-0.5), reads=[Bst_h[i2]], writes=[Bst_h[i2]])
        S.op('dve', lambda e: e.scalar_tensor_tensor(out=xn[i2], in0=src_ap, scalar=rr, in1=gpre_bc, op0=ALU.mult, op1=ALU.mult),
             reads=[Bsrc, Bst_h[i2], Bgpre], writes=[Bxn[i2]])

    def hT_part2(tt, hTdst, Bdst_list, i2):
        pst = psum_bf[:, 6 * 1024: 8 * 1024]
        fns = [lambda e, kc=kc: e.transpose(pst[:, kc * 128:(kc + 1) * 128], xn[i2][:, kc * 128:(kc + 1) * 128], ident) for kc in range(16)]
        S.op('pe', fns, reads=[Bxn[i2], Bconst], writes=[Bbank[6], Bbank[7]])
        h3 = v3(hTdst, 16)
        S.op('dve', lambda e: e.tensor_copy(out=h3[:, :, tt * 128:(tt + 1) * 128], in_=pst.rearrange("p (k n) -> p k n", k=16)),
             reads=[Bbank[6], Bbank[7]], writes=Bdst_list)

    def make_hT_tile(src_ap, Bsrc, tt, hTdst, Bdst_list, li, i2, scol):
        hT_part1(src_ap, Bsrc, i2, scol)
        hT_part2(tt, hTdst, Bdst_list, i2)

    def load_gains(li, with_post):
        S.dma('sp', lambda e: e.dma_start(out=gpre_bc, in_=gpre_d[li:li + 1, :].broadcast_to([128, D])), 'gpre', writes=[Bgpre])
        if with_post:
            S.dma('sp', lambda e: e.dma_start(out=gpost_bc, in_=gpost_d[li - 1:li, :].broadcast_to([128, D])), 'gpost', writes=[Bgpost])

    hp = 0
    load_gains(0, False)
    w_issue_upto(NW - 1)
    xb4 = [xt[0], xt[1], yraw[0], yraw[1]]
    Bxb4 = [Bxt[0], Bxt[1], Byraw[0], Byraw[1]]
    xkeys = ['xt0', 'xt1', 'xp2', 'xp3']
    for tt in range(16):
        i2 = tt % 2
        i4 = tt % 4
        S.dma('sp', lambda e, tt=tt, i4=i4: e.dma_start(out=xb4[i4], in_=x_d[tt * 128:(tt + 1) * 128, :]), xkeys[i4], writes=[Bxb4[i4]])
        hT_part1(xb4[i4], Bxb4[i4], i2, 4 * i2)
        if tt > 0:
            hT_part2(tt - 1, bufs2[hp], [BhT[hp][tt - 1]], (tt - 1) % 2)
    hT_part2(15, bufs2[hp], [BhT[hp][15]], 1)

    if 'hT' in dbg_d:
        S.barrier()
        for kc in range(16):
            S.op('dve', lambda e, kc=kc: e.tensor_copy(out=yraw[0], in_=v3(bufs2[hp], 16)[:, kc, :]), reads=BhT[hp], writes=[Byraw[0]])
            S.dma('sp', lambda e, kc=kc: e.dma_start(out=dbg_d['hT'][kc * 128:(kc + 1) * 128, :], in_=yraw[0]), 'dbg', reads=[Byraw[0]])

    def do_layer(li, hp):
        if stop_after == 'pre':
            return False
        hT = bufs2[hp]
        oT = bufs2[1 - hp]
        BhTl = BhT[hp]
        BoTl = BoT[1 - hp]
        vb = li * 8
        S.barrier()
        gbase = li * NBLK

        o3 = v3(oT, 16)
        cqn = oT[:, 12 * 2048:16 * 2048]
        ckvn = oT[:, 0:2 * 2048]
        kpe = oT[:, 2 * 2048:3 * 2048]
        qpe = [oT[:, 3 * 2048:4 * 2048], oT[:, 4 * 2048:5 * 2048]]
        wuq_sb = oT[:, 5 * 2048:5 * 2048 + 3072]
        wukv_sb = oT[:, 7 * 2048:8 * 2048]
        Bcqn = [Buf("cqn%d" % c) for c in range(4)]
        Bckvn = [Buf("ckvn%d" % c) for c in range(4)]
        Bkpe = Buf("kpe")
        Bqpe = [[Buf("qpe%d_%d" % (i, c)) for c in range(4)] for i in range(2)]
        Bwu = Buf("wu")
        alias_all = [BoTl[m][c] for m in list(range(0, 8)) + list(range(12, 16)) for c in range(4)]
        S.dma('pool', lambda e: e.dma_start(out=wuq_sb, in_=wuq_d[li * 128:(li + 1) * 128, :]), 'wu', writes=[Bwu] + alias_all)
        S.dma('pool', lambda e: e.dma_start(out=wukv_sb, in_=wukv_d[li * 128:(li + 1) * 128, :]), 'wu', writes=[Bwu])
        wuq3 = v3(wuq_sb, 4)
        wukv3 = v3(wukv_sb, 2)

        def lowrank_norm(gi0, nblk, dstbuf, Bdst, gcol, n):
            d3 = v3(dstbuf, nblk)
            for blk in range(nblk):
                def consume(tc, bk, blk=blk):
                    i2 = tc % 2
                    S.op('act', lambda e: e.activation(out=sqs[i2], in_=bank(bk), func=AF.Square), reads=[Bbank[bk]], writes=[Bsq[i2]])
                    S.op('dve', lambda e: e.tensor_copy(out=d3[:, blk, tc * 512:(tc + 1) * 512], in_=bank(bk)), reads=[Bbank[bk]], writes=[Bdst[tc]])
                    S.op('pe', lambda e: e.matmul(bank(tc), ones, sqs[i2], start=(blk == 0), stop=(blk == nblk - 1)),
                         reads=[Bsq[i2], Bconst], writes=[Bbank[tc]])
                proj_block_fm(hT, BhTl, gi0 + blk, 128, consume)
            flush_pend()
            for tc in range(4):
                i2 = tc % 2
                rstd_from_bank(tc, 128, rsb[i2], Brs[i2], float(n))
                for blk in range(nblk):
                    S.op('dve', lambda e, blk=blk, tc=tc, i2=i2: e.scalar_tensor_tensor(
                        out=d3[:, blk, tc * 512:(tc + 1) * 512], in0=d3[:, blk, tc * 512:(tc + 1) * 512],
                        scalar=vecs[:, vb + gcol + blk: vb + gcol + blk + 1], in1=rsb[i2], op0=ALU.mult, op1=ALU.mult),
                        reads=[Bdst[tc], Brs[i2], Bconst], writes=[Bdst[tc]])

        lowrank_norm(gbase + 0, 4, cqn, Bcqn, 2, 512)
        if stop_after == 'B1':
            return False
        lowrank_norm(gbase + 4, 2, ckvn, Bckvn, 6, 256)
        proj_block_fm(hT, BhTl, gbase + 6, 64,
                      lambda tc, bk: rope_chunk(bk, 64, 32, tc, TBc, TBs, permB, None, 0, kpe[0:64, tc * 512:(tc + 1) * 512], Bkpe, tc % 2))

        def prepB(h, slot):
            proj_block_fm(hT, BhTl, gbase + 7 + h, 128,
                          lambda tc, bk: silu_chunk(bk, gT[slot][:, tc * 512:(tc + 1) * 512], BgT[slot][tc], tc % 2))
            for tc in range(4):
                bk = next_pp()
                fns = [lambda e, kc=kc, bk=bk, tc=tc: e.matmul(bank(bk), wuq3[:, kc, h * 192:h * 192 + 128], v3(cqn, 4)[:, kc, tc * 512:(tc + 1) * 512],
                                                               start=(kc == 0), stop=(kc == 3)) for kc in range(4)]
                S.op('pe', fns, reads=[Bwu, Bcqn[tc]], writes=[Bbank[bk]])
                flush_pend()
                S.op('act', lambda e, bk=bk, tc=tc: e.activation(out=qT[slot][:, tc * 512:(tc + 1) * 512], in_=bank(bk), func=AF.Copy),
                     reads=[Bbank[bk]], writes=[BqT[slot][tc]])
                bk = next_pp()
                fns = [lambda e, kc=kc, bk=bk, tc=tc: e.matmul(bank(bk)[0:64, :], wuq3[:, kc, h * 192 + 128:h * 192 + 192], v3(cqn, 4)[:, kc, tc * 512:(tc + 1) * 512],
                                                               start=(kc == 0), stop=(kc == 3)) for kc in range(4)]
                S.op('pe', fns, reads=[Bwu, Bcqn[tc]], writes=[Bbank[bk]])
                rope_chunk(bk, 64, 32, tc, TBc, TBs, permB, None, 0, qpe[slot][0:64, tc * 512:(tc + 1) * 512], Bqpe[slot][tc], tc % 2)

        def prepB_kv(h):
            for tc in range(4):
                bk = next_pp()
                fns = [lambda e, kc=kc, bk=bk, tc=tc: e.matmul(bank(bk), wukv3[:, kc, h * 256:h * 256 + 128], v3(ckvn, 2)[:, kc, tc * 512:(tc + 1) * 512],
                                                               start=(kc == 0), stop=(kc == 1)) for kc in range(2)]
                S.op('pe', fns, reads=[Bwu, Bckvn[tc]], writes=[Bbank[bk]])
                flush_pend()
                S.op('act', lambda e, bk=bk, tc=tc: e.activation(out=kT[:, tc * 512:(tc + 1) * 512], in_=bank(bk), func=AF.Copy),
                     reads=[Bbank[bk]], writes=[BkT[tc]])
            for tq in range(4):
                bk = next_pp()
                fns = []
                for j in range(4):
                    tt = tq * 4 + j
                    for kc in range(2):
                        fns.append(lambda e, kc=kc, tt=tt, j=j, bk=bk: e.matmul(bank(bk)[:, j * 128:(j + 1) * 128], v3(ckvn, 2)[:, kc, tt * 128:(tt + 1) * 128],
                                                                                wukv3[:, kc, h * 256 + 128:h * 256 + 256], start=(kc == 0), stop=(kc == 1)))
                S.op('pe', fns, reads=[Bwu, Bckvn[tq]], writes=[Bbank[bk]])
                S.op('act', lambda e, bk=bk, tq=tq: e.activation(out=Vt[:, tq * 512:(tq + 1) * 512], in_=bank(bk), func=AF.Copy),
                     reads=[Bbank[bk]], writes=[BVt[tq]])

        if stop_after == 'B2':
            return False
        scaleB = float(192 ** -0.5)
        prepB(0, 0)
        if stop_after == 'B3':
            return False
        if stop_after == 'B4':
            prepB_kv(0)
            return False
        for h in range(4):
            if h + 1 < 4:
                prepB(h + 1, (h + 1) % 2)
            prepB_kv(h)
            attn_dense(oT, BoTl, 8 + h, h % 2, h % 2, scaleB, kpe=kpe, qpe=qpe[h % 2], Bkpe=Bkpe, Bqpe=Bqpe[h % 2])
        if stop_after == 'B':
            return False

        Balias = Bcqn + Bckvn + [Bkpe, Bwu] + Bqpe[0] + Bqpe[1]

        def alias_guard(mc):
            for b in Balias:
                for c in range(4):
                    for k, v in b.r.items():
                        if BoTl[mc][c].r.get(k, 0) < v:
                            BoTl[mc][c].r[k] = v
                    if b.w is not None:
                        k, v = b.w
                        if BoTl[mc][c].r.get(k, 0) < v:
                            BoTl[mc][c].r[k] = v
        for mc in list(range(0, 8)) + list(range(12, 16)):
            alias_guard(mc)

        S.dma('sp', lambda e: e.dma_start(out=cmask, in_=cmask_d), 'cmask', writes=[Bcmask])
        scaleC = float(128 ** -0.5)

        def prepC_bias(h, bslot):
            S.dma('pool', lambda e: e.dma_start(out=biasm[bslot], in_=cbias_d[(li * 4 + h) * 128:(li * 4 + h + 1) * 128, :]), 'bias%d' % bslot, writes=[Bbiasm[bslot]])
            S.op('pool', lambda e: e.tensor_tensor(out=biasm[bslot], in0=biasm[bslot], in1=cmask, op=ALU.add), reads=[Bbiasm[bslot], Bcmask], writes=[Bbiasm[bslot]])

        def prepC_q(h, slot):
            g0 = gbase + 11 + 4 * h
            proj_block_fm(hT, BhTl, g0, 128,
                          lambda tc, bk: S.op('act', lambda e: e.activation(out=qT[slot][:, tc * 512:(tc + 1) * 512], in_=bank(bk), func=AF.Copy, scale=scaleC),
                                              reads=[Bbank[bk]], writes=[BqT[slot][tc]]))

        def prepC_kvg(h, slot):
            g0 = gbase + 11 + 4 * h
            proj_block_fm(hT, BhTl, g0 + 1, 128,
                          lambda tc, bk: S.op('dve', lambda e: e.tensor_copy(out=kT[:, tc * 512:(tc + 1) * 512], in_=bank(bk)),
                                              reads=[Bbank[bk]], writes=[BkT[tc]]))
            proj_block_tm(hT, BhTl, g0 + 2, Vt, BVt)
            proj_block_fm(hT, BhTl, g0 + 3, 128,
                          lambda tc, bk: silu_chunk(bk, gT[slot][:, tc * 512:(tc + 1) * 512], BgT[slot][tc], tc % 2))

        for h in range(4):
            prepC_bias(h, h % 2)
            prepC_q(h, h % 2)
            if stop_after == 'C1':
                break
            prepC_kvg(h, h % 2)
            if stop_after == 'C2':
                break
            attn_window(oT, BoTl, 12 + h, h % 2, h % 2, h % 2)
            if stop_after == 'C3':
                break
        if stop_after in ('C1', 'C2', 'C3'):
            return False
        if stop_after == 'C':
            return False

        scaleA = float(128 ** 0.5)
        aq = vecs[:, vb + 0:vb + 1]
        ak = vecs[:, vb + 1:vb + 2]

        def prepA_kv(g):
            g0 = gbase + 27 + 10 * g
            proj_block_fm(hT, BhTl, g0, 128,
                          lambda tc, bk: rope_chunk(bk, 128, 64, tc, TAc, TAs, permA, ak, 128.0, kT[:, tc * 512:(tc + 1) * 512], BkT[tc], tc % 2))
            proj_block_tm(hT, BhTl, g0 + 1, Vt, BVt)

        def prepA_q(h, slot):
            g = h // 4
            g0 = gbase + 27 + 10 * g + 2 + 2 * (h % 4)
            proj_block_fm(hT, BhTl, g0, 128,
                          lambda tc, bk: rope_chunk(bk, 128, 64, tc, TAc, TAs, permA, aq, 128.0, qT[slot][:, tc * 512:(tc + 1) * 512], BqT[slot][tc], tc % 2))
            proj_block_fm(hT, BhTl, g0 + 1, 128,
                          lambda tc, bk: silu_chunk(bk, gT[slot][:, tc * 512:(tc + 1) * 512], BgT[slot][tc], tc % 2))

        w3o = v3(hT, 16)
        wout_issued = {'v': False}

        def issue_wout():
            wout_issued['v'] = True
            for mc in range(16):
                S.dma('pool', lambda e, mc=mc: e.dma_start(out=w3o[:, mc, :], in_=wout_d[(li * 16 + mc) * 128:(li * 16 + mc + 1) * 128, :]),
                      'wout', writes=BhTl if mc == 0 else [])

        for g in range(2):
            prepA_kv(g)
            prepA_q(4 * g, 0)
            for hh in range(4):
                h = 4 * g + hh
                if hh + 1 < 4:
                    prepA_q(h + 1, (hh + 1) % 2)
                    if h + 1 == 7:
                        issue_wout()
                attn_dense(oT, BoTl, h, hh % 2, hh % 2, scaleA)
        if stop_after == 'A':
            return False

        flush_pend()
        if not wout_issued['v']:
            issue_wout()
        Bwout = Buf("wout")
        Bwout.w = ('wout', S.dcnt['wout'])
        S.barrier()
        last = (li == L - 1)
        load_gains(li + 1 if not last else li, True) if not last else \
            S.dma('sp', lambda e: e.dma_start(out=gpost_bc, in_=gpost_d[li:li + 1, :].broadcast_to([128, D])), 'gpost', writes=[Bgpost])
        src_d = x_d if li == 0 else out_d
        o3 = v3(oT, 16)
        ybank = {'i': 0}
        Bjunk = Buf("junk")
        pend1 = None
        pend2 = None
        Bot = [Buf("ot%d" % t) for t in range(16)]
        for tt in range(16):
            i2 = tt % 2
            tcq = tt // 4
            S.dma('sp', lambda e, tt=tt, i2=i2: e.dma_start(out=xt[i2], in_=src_d[tt * 128:(tt + 1) * 128, :]), 'xt%d' % i2,
                  reads=[Bout[tt]], writes=[Bxt[i2]])
            ssp = stat[:, 16 + 8 * i2:16 + 8 * i2 + 4]
            for nb in range(4):
                bk = ybank['i'] % 6
                ybank['i'] += 1
                fns = [lambda e, mc=mc, nb=nb, bk=bk, tt=tt: e.matmul(bank(bk), o3[:, mc, tt * 128:(tt + 1) * 128], w3o[:, mc, nb * 512:(nb + 1) * 512],
                                                                      start=(mc == 0), stop=(mc == 15)) for mc in range(16)]
                S.op('pe', fns, reads=[Bwout, Bot[tt]], writes=[Bbank[bk]])
                S.op('act', lambda e, nb=nb, bk=bk, i2=i2, ssp=ssp: e.activation(out=junk, in_=bank(bk), func=AF.Square, accum_out=ssp[:, nb:nb + 1]),
                     reads=[Bbank[bk]], writes=[Bjunk, Bst_y[i2]], fuse=False)
                S.op('dve', lambda e, nb=nb, bk=bk, i2=i2: e.tensor_copy(out=yraw[i2][:, nb * 512:(nb + 1) * 512], in_=bank(bk)),
                     reads=[Bbank[bk]], writes=[Byraw[i2]])
            ss = stat[:, 32 + 4 * i2:32 + 4 * i2 + 1]
            rr = stat[:, 32 + 4 * i2 + 1:32 + 4 * i2 + 2]
            S.op('dve', lambda e, ss=ss, ssp=ssp: e.tensor_reduce(out=ss, in_=ssp, axis=mybir.AxisListType.X, op=ALU.add), reads=[Bst_y[i2]], writes=[Bst_y[i2]], fuse=False)
            S.op('act', lambda e, ss=ss, rr=rr: e.activation(out=rr, in_=ss, func=AF.Ln, scale=1.0 / D, bias=EPS), reads=[Bst_y[i2]], writes=[Bst_y[i2]])
            S.op('act', lambda e, rr=rr: e.activation(out=rr, in_=rr, func=AF.Exp, scale=-0.5), reads=[Bst_y[i2]], writes=[Bst_y[i2]])
            S.op('dve', lambda e, i2=i2, rr=rr: e.scalar_tensor_tensor(out=yraw[i2], in0=yraw[i2], scalar=rr, in1=gpost_bc, op0=ALU.mult, op1=ALU.mult),
                 reads=[Byraw[i2], Bst_y[i2], Bgpost], writes=[Byraw[i2]])
            S.op('pool', lambda e, i2=i2: e.tensor_tensor(out=xt[i2], in0=xt[i2], in1=yraw[i2], op=ALU.add), reads=[Bxt[i2], Byraw[i2]], writes=[Bxt[i2]])
            S.dma('sp', lambda e, tt=tt, i2=i2: e.dma_start(out=out_d[tt * 128:(tt + 1) * 128, :], in_=xt[i2]), 'st%d' % i2,
                  reads=[Bxt[i2]], writes=[Bout[tt]])
            if not last:
                if pend2 is not None:
                    hT_part2(pend2, oT, [BhT[1 - hp][pend2], Bot[pend2]], pend2 % 2)
                    pend2 = None
                if pend1 is not None:
                    hT_part1(xt[pend1 % 2], Bxt[pend1 % 2], pend1 % 2, 4 * (pend1 % 2))
                    pend2 = pend1
                pend1 = tt
        if not last:
            if pend2 is not None:
                hT_part2(pend2, oT, [BhT[1 - hp][pend2], Bot[pend2]], pend2 % 2)
            hT_part1(xt[pend1 % 2], Bxt[pend1 % 2], pend1 % 2, 4 * (pend1 % 2))
            hT_part2(pend1, oT, [BhT[1 - hp][pend1], Bot[pend1]], pend1 % 2)
        return True


    for li in range(L):
        if not do_layer(li, hp):
            break
        hp = 1 - hp

    if 'oT' in dbg_d:
        S.barrier()
        src = bufs2[1]
        for kc in dbg.get('oT_chunks', range(16)):
            S.op('dve', lambda e, kc=kc: e.tensor_copy(out=yraw[0], in_=v3(src, 16)[:, kc, :]), reads=[], writes=[Byraw[0]])
            S.dma('sp', lambda e, kc=kc: e.dma_start(out=dbg_d['oT'][kc * 128:(kc + 1) * 128, :], in_=yraw[0]), 'dbg', reads=[Byraw[0]])
    S.barrier()
    S.emit(nc)
    st.close()
    return nc


def _prep_shared(norm_pre, norm_post, w_in, a_q_norm, a_k_norm, b_q_norm, b_kv_norm, b_w_uq, b_w_ukv, c_rpb, w_out):
    L = w_in.shape[0]
    cols = _block_cols()
    win = np.empty((L, NBLK, 128, 16, 128), np.float32)
    for l in range(L):
        g = w_in[l][:, cols.reshape(-1)].reshape(16, 128, NBLK, 128)
        win[l] = g.transpose(2, 1, 0, 3)
    win = win.reshape(L * NBLK * 128, 2048)
    wuq = np.ascontiguousarray(b_w_uq.reshape(L, 4, 128, 768).transpose(0, 2, 1, 3)).reshape(L * 128, 4 * 768)
    wukv = np.ascontiguousarray(b_w_ukv.reshape(L, 2, 128, 1024).transpose(0, 2, 1, 3)).reshape(L * 128, 2 * 1024)
    wout = np.ascontiguousarray(w_out.reshape(L * 16 * 128, 2048))
    vecs = np.zeros((128, L * 8), np.float32)
    for l in range(L):
        vecs[:, l * 8 + 0] = a_q_norm[l]
        vecs[:, l * 8 + 1] = a_k_norm[l]
        vecs[:, l * 8 + 2:l * 8 + 6] = b_q_norm[l].reshape(4, 128).T
        vecs[:, l * 8 + 6:l * 8 + 8] = b_kv_norm[l].reshape(2, 128).T
    dr, dc, va = _c_bias_index()
    cb = c_rpb[:, :, dr, dc]
    cb = np.where(va[None, None], cb, np.float32(0.0))
    cbias = np.ascontiguousarray(cb.transpose(0, 1, 3, 2, 4)).reshape(L * 4 * 128, 21 * 128).astype(np.float32)
    cmats, tabs, cmask = _const_tables()
    return {"win": win, "wuq": wuq, "wukv": wukv, "wout": wout,
            "gpre": np.ascontiguousarray(norm_pre, dtype=np.float32), "gpost": np.ascontiguousarray(norm_post, dtype=np.float32),
            "vecs": vecs, "cbias": cbias, "cmask": cmask, "cmats": cmats, "ctabs": tabs}


def kernel(x, norm_pre, norm_post, w_in, a_q_norm, a_k_norm, b_q_norm, b_kv_norm, b_w_uq, b_w_ukv, c_rpb, w_out):
    args = [np.asarray(a, dtype=np.float32) for a in (norm_pre, norm_post, w_in, a_q_norm, a_k_norm, b_q_norm, b_kv_norm, b_w_uq, b_w_ukv, c_rpb, w_out)]
    x = np.asarray(x, dtype=np.float32)
    shared = _prep_shared(*args)
    nc = bass.Bass("TRN2", target_bir_lowering=False)
    build(nc, L=DEPTH)
    n = x.shape[0]
    in_maps = []
    for b in range(n):
        m = dict(shared)
        m["x"] = np.ascontiguousarray(x[b])
        in_maps.append(m)
    res = run_bass_kernel_spmd(nc, in_maps, core_ids=list(range(n)))
    return np.stack([np.asarray(r["out"], dtype=np.float32) for r in res.results], axis=0)
```
